# Optimizing a Trainium2 kernel written in Bass

```python
import math
import jax, jax.numpy as jnp
from jax import lax
import numpy as np

D_MODEL = 1024
BATCH = 16
SEQ = 4096
DEPTH = 2

W_LRU = D_MODEL // 2
LRU_BLOCKS = 8
LRU_C = 8.0
CONV_K = 4

SSM_HEAD_DIM = 64
SSM_D_INNER = D_MODEL
SSM_HEADS = SSM_D_INNER // SSM_HEAD_DIM
SSM_GROUPS = 2
SSM_STATE = 128
SSM_CHUNK = 128
SSM_CONV_CH = SSM_D_INNER + 2 * SSM_GROUPS * SSM_STATE

ATTN_HEAD_DIM = 64
ATTN_Q_HEADS = (D_MODEL // 2) // ATTN_HEAD_DIM
ATTN_KV_HEADS = 2
WINDOW = 128
ATTN_BLOCK = WINDOW

MIX_WIDTH = W_LRU + SSM_D_INNER + ATTN_Q_HEADS * ATTN_HEAD_DIM

_SEG = (W_LRU, W_LRU,
        SSM_D_INNER, SSM_CONV_CH, SSM_HEADS,
        ATTN_Q_HEADS * ATTN_HEAD_DIM,
        ATTN_KV_HEADS * ATTN_HEAD_DIM,
        ATTN_KV_HEADS * ATTN_HEAD_DIM)
IN_COLS = sum(_SEG)
SPLIT_POINTS = tuple(int(v) for v in np.cumsum(_SEG)[:-1])

PEER_HEADS = 8
PEER_NKEYS = 128
PEER_EXPERTS = PEER_NKEYS * PEER_NKEYS
PEER_DKEY = 128
PEER_TOPK = 16
PEER_CHUNK = 128

DN_ALPHA = (2 * DEPTH) ** 0.25
DN_BETA = (8 * DEPTH) ** -0.25
LN_EPS = 1e-5

kernel_name = "hymba_style_rglru_ssd_swa_peer_deepnorm"


def layer_norm(x, g, b):
    xf = x.astype(jnp.float32)
    mu = jnp.mean(xf, axis=-1, keepdims=True)
    var = jnp.mean(jnp.square(xf - mu), axis=-1, keepdims=True)
    return ((xf - mu) * lax.rsqrt(var + LN_EPS) * g + b).astype(x.dtype)


def causal_dwconv(x, w, b):
    k = w.shape[0]
    s = x.shape[1]
    xp = jnp.pad(x, ((0, 0), (k - 1, 0), (0, 0)))
    out = b + w[0] * xp[:, 0:s]
    for i in range(1, k):
        out = out + w[i] * xp[:, i:i + s]
    return out


def rg_lru(x, w_a, b_a, w_x, b_x, lam):
    bsz, s, w = x.shape
    xb = x.reshape(bsz, s, LRU_BLOCKS, w // LRU_BLOCKS)
    r = jax.nn.sigmoid((jnp.einsum('bsnc,ncd->bsnd', xb, w_a).reshape(bsz, s, w) + b_a).astype(jnp.float32))
    i = jax.nn.sigmoid((jnp.einsum('bsnc,ncd->bsnd', xb, w_x).reshape(bsz, s, w) + b_x).astype(jnp.float32))
    log_a = -LRU_C * r * jax.nn.softplus(-lam.astype(jnp.float32))
    a = jnp.exp(log_a)
    u = jnp.sqrt(-jnp.expm1(2.0 * log_a)) * (i * x.astype(jnp.float32))

    def combine(left, right):
        a_l, h_l = left
        a_r, h_r = right
        return a_l * a_r, a_r * h_l + h_r

    _, h = lax.associative_scan(combine, (a, u), axis=1)
    return h.astype(x.dtype)


def ssd_scan(xh, dt, a, bmat, cmat):
    f32 = jnp.float32
    bsz, s, h, p = xh.shape
    g, n = bmat.shape[2], bmat.shape[3]
    hg = h // g
    nc, l = s // SSM_CHUNK, SSM_CHUNK
    x = (xh.astype(f32) * dt[..., None]).reshape(bsz, nc, l, g, hg, p)
    da = (dt * a).reshape(bsz, nc, l, g, hg).transpose(0, 3, 4, 1, 2)
    bc = bmat.astype(f32).reshape(bsz, nc, l, g, n)
    cc = cmat.astype(f32).reshape(bsz, nc, l, g, n)
    a_cs = jnp.cumsum(da, axis=-1)
    causal = jnp.tril(jnp.ones((l, l), dtype=bool))
    seg = a_cs[..., :, None] - a_cs[..., None, :]
    decay = jnp.exp(jnp.where(causal, seg, -jnp.inf))
    cb = jnp.einsum('bclgn,bcsgn->bgcls', cc, bc)
    y_diag = jnp.einsum('bghcls,bcsghp->bclghp', cb[:, :, None] * decay, x)
    decay_states = jnp.exp(a_cs[..., -1:] - a_cs).transpose(0, 3, 4, 1, 2)
    states = jnp.einsum('bclgn,bclghp->bcghpn', bc, x * decay_states[..., None])
    chunk_decay = jnp.exp(a_cs[..., -1])

    def step(carry, inp):
        st, dec = inp
        return carry * dec[..., None, None] + st, carry

    init = jnp.zeros((bsz, g, hg, p, n), f32)
    _, prev = lax.scan(step, init, (jnp.moveaxis(states, 1, 0), jnp.moveaxis(chunk_decay, 3, 0)))
    prev = jnp.moveaxis(prev, 0, 1)
    out_decay = jnp.exp(a_cs).transpose(0, 3, 4, 1, 2)
    y_off = jnp.einsum('bclgn,bcghpn->bclghp', cc, prev) * out_decay[..., None]
    return (y_diag + y_off).reshape(bsz, s, h, p)


def mamba2_group(z, xbc, dt_raw, conv_w, conv_b, dt_bias, a_log, d_skip, norm_g):
    f32 = jnp.float32
    bsz, s, _ = z.shape
    xbc = jax.nn.silu(causal_dwconv(xbc, conv_w, conv_b))
    xs, bm, cm = jnp.split(xbc, [SSM_D_INNER, SSM_D_INNER + SSM_GROUPS * SSM_STATE], axis=-1)
    dt = jax.nn.softplus(dt_raw.astype(f32) + dt_bias.astype(f32))
    a = -jnp.exp(a_log.astype(f32))
    xh = xs.reshape(bsz, s, SSM_HEADS, SSM_HEAD_DIM)
    y = ssd_scan(xh, dt, a,
                 bm.reshape(bsz, s, SSM_GROUPS, SSM_STATE),
                 cm.reshape(bsz, s, SSM_GROUPS, SSM_STATE))
    y = y + d_skip.astype(f32)[:, None] * xh.astype(f32)
    y = y.reshape(bsz, s, SSM_D_INNER) * jax.nn.silu(z.astype(f32))
    yg = y.reshape(bsz, s, SSM_GROUPS, SSM_D_INNER // SSM_GROUPS)
    yg = yg * lax.rsqrt(jnp.mean(jnp.square(yg), axis=-1, keepdims=True) + LN_EPS)
    return (yg.reshape(bsz, s, SSM_D_INNER) * norm_g).astype(z.dtype)


def sliding_window_attention(q, k, v, sinks):
    f32 = jnp.float32
    bsz, s, _ = q.shape
    nb = s // ATTN_BLOCK
    rep = ATTN_Q_HEADS // ATTN_KV_HEADS
    qb = q.reshape(bsz, nb, ATTN_BLOCK, ATTN_KV_HEADS, rep, ATTN_HEAD_DIM)
    kb = k.reshape(bsz, nb, ATTN_BLOCK, ATTN_KV_HEADS, ATTN_HEAD_DIM)
    vb = v.reshape(bsz, nb, ATTN_BLOCK, ATTN_KV_HEADS, ATTN_HEAD_DIM)

    def with_prev(t):
        prev = jnp.pad(t, ((0, 0), (1, 0), (0, 0), (0, 0), (0, 0)))[:, :-1]
        return jnp.concatenate([prev, t], axis=2)

    kk, vv = with_prev(kb), with_prev(vb)
    logits = jnp.einsum('bnqhrd,bnkhd->bnhrqk', qb, kk).astype(f32) * (ATTN_HEAD_DIM ** -0.5)
    qi = jnp.arange(ATTN_BLOCK)[:, None]
    kj = jnp.arange(2 * ATTN_BLOCK)[None, :]
    rel = qi + ATTN_BLOCK - kj
    band = (rel >= 0) & (rel < WINDOW)
    blk = jnp.arange(nb)[:, None, None]
    valid = band[None] & ((blk > 0) | (kj[None] >= ATTN_BLOCK))
    logits = jnp.where(valid[None, :, None, None], logits, -jnp.inf)
    sink = sinks.astype(f32).reshape(ATTN_KV_HEADS, rep)[None, None, :, :, None, None]
    m = jnp.maximum(jnp.max(logits, axis=-1, keepdims=True), sink)
    p = jnp.exp(logits - m)
    probs = p / (jnp.sum(p, axis=-1, keepdims=True) + jnp.exp(sink - m))
    out = jnp.einsum('bnhrqk,bnkhd->bnqhrd', probs.astype(v.dtype), vv)
    return out.reshape(bsz, s, ATTN_Q_HEADS * ATTN_HEAD_DIM)


def peer_ffn(x, wq, keys, u, v):
    bsz, s, d = x.shape
    xt = x.reshape(-1, PEER_CHUNK, d)
    half = PEER_DKEY // 2

    def chunk(xc):
        c = xc.shape[0]
        q = (xc @ wq).reshape(c, PEER_HEADS, 2, half)
        sc = jnp.einsum('thid,ikd->thik', q, keys).astype(jnp.float32)
        top_s, top_i = lax.top_k(sc, PEER_TOPK)
        cand_s = top_s[:, :, 0, :, None] + top_s[:, :, 1, None, :]
        cand_i = top_i[:, :, 0, :, None] * PEER_NKEYS + top_i[:, :, 1, None, :]
        best_s, best_pos = lax.top_k(cand_s.reshape(c, PEER_HEADS, -1), PEER_TOPK)
        idx = jnp.take_along_axis(cand_i.reshape(c, PEER_HEADS, -1), best_pos, axis=-1)
        gate = jax.nn.softmax(best_s, axis=-1)
        ue = u[idx]
        ve = v[idx]
        act = jax.nn.gelu(jnp.einsum('td,thkd->thk', xc, ue).astype(jnp.float32), approximate=False)
        return jnp.einsum('thk,thkd->td', (gate * act).astype(xc.dtype), ve)

    return lax.map(chunk, xt).reshape(bsz, s, d)


def setup_inputs(seed: int = 0) -> dict:
    key = jax.random.key(seed)
    ks = jax.random.split(key, 32)
    f32 = jnp.float32

    def nrm(k, shape, scale):
        return jax.random.normal(k, shape, f32) * scale

    bw = W_LRU // LRU_BLOCKS
    a0 = jax.random.uniform(ks[10], (DEPTH, W_LRU), f32, 0.9, 0.999)
    a_base = a0 ** (1.0 / LRU_C)
    dt0 = jnp.exp(jax.random.uniform(ks[13], (DEPTH, SSM_HEADS), f32, math.log(1e-3), math.log(1e-1)))
    return {
        "x": nrm(ks[0], (BATCH, SEQ, D_MODEL), 1.0),
        "emb_ln_g": 1.0 + nrm(ks[1], (D_MODEL,), 0.02),
        "emb_ln_b": nrm(ks[2], (D_MODEL,), 0.02),
        "w_in": nrm(ks[3], (DEPTH, D_MODEL, IN_COLS), D_MODEL ** -0.5),
        "rg_conv_w": nrm(ks[4], (DEPTH, CONV_K, W_LRU), CONV_K ** -0.5),
        "rg_conv_b": nrm(ks[5], (DEPTH, W_LRU), 0.02),
        "rg_wa": nrm(ks[6], (DEPTH, LRU_BLOCKS, bw, bw), bw ** -0.5),
        "rg_ba": nrm(ks[7], (DEPTH, W_LRU), 0.02),
        "rg_wx": nrm(ks[8], (DEPTH, LRU_BLOCKS, bw, bw), bw ** -0.5),
        "rg_bx": nrm(ks[9], (DEPTH, W_LRU), 0.02),
        "rg_lambda": jnp.log(a_base) - jnp.log1p(-a_base),
        "ssm_conv_w": nrm(ks[11], (DEPTH, CONV_K, SSM_CONV_CH), CONV_K ** -0.5),
        "ssm_conv_b": nrm(ks[12], (DEPTH, SSM_CONV_CH), 0.02),
        "ssm_dt_bias": dt0 + jnp.log(-jnp.expm1(-dt0)),
        "ssm_a_log": jnp.log(jax.random.uniform(ks[14], (DEPTH, SSM_HEADS), f32, 1.0, 16.0)),
        "ssm_d": 1.0 + nrm(ks[15], (DEPTH, SSM_HEADS), 0.02),
        "ssm_norm_g": 1.0 + nrm(ks[16], (DEPTH, SSM_D_INNER), 0.02),
        "attn_sinks": nrm(ks[17], (DEPTH, ATTN_Q_HEADS), 1.0),
        "w_out": nrm(ks[18], (DEPTH, MIX_WIDTH, D_MODEL), DN_BETA * MIX_WIDTH ** -0.5),
        "ln1_g": 1.0 + nrm(ks[19], (DEPTH, D_MODEL), 0.02),
        "ln1_b": nrm(ks[20], (DEPTH, D_MODEL), 0.02),
        "peer_wq": nrm(ks[21], (DEPTH, D_MODEL, PEER_HEADS * PEER_DKEY), D_MODEL ** -0.5),
        "peer_keys": nrm(ks[22], (DEPTH, 2, PEER_NKEYS, PEER_DKEY // 2), (PEER_DKEY // 2) ** -0.5),
        "peer_u": nrm(ks[23], (DEPTH, PEER_EXPERTS, D_MODEL), D_MODEL ** -0.5),
        "peer_v": nrm(ks[24], (DEPTH, PEER_EXPERTS, D_MODEL), DN_BETA * PEER_HEADS ** -0.5),
        "ln2_g": 1.0 + nrm(ks[25], (DEPTH, D_MODEL), 0.02),
        "ln2_b": nrm(ks[26], (DEPTH, D_MODEL), 0.02),
    }


def reference(x, emb_ln_g, emb_ln_b, w_in, rg_conv_w, rg_conv_b, rg_wa, rg_ba, rg_wx, rg_bx,
              rg_lambda, ssm_conv_w, ssm_conv_b, ssm_dt_bias, ssm_a_log, ssm_d, ssm_norm_g,
              attn_sinks, w_out, ln1_g, ln1_b, peer_wq, peer_keys, peer_u, peer_v, ln2_g, ln2_b):
    h = layer_norm(x, emb_ln_g, emb_ln_b)
    for l in range(DEPTH):
        proj = h @ w_in[l]
        rg_x, rg_gate, z, xbc, dt_raw, q, k, v = jnp.split(proj, SPLIT_POINTS, axis=-1)
        y_a = jax.nn.gelu(rg_gate) * rg_lru(causal_dwconv(rg_x, rg_conv_w[l], rg_conv_b[l]),
                                            rg_wa[l], rg_ba[l], rg_wx[l], rg_bx[l], rg_lambda[l])
        y_b = mamba2_group(z, xbc, dt_raw, ssm_conv_w[l], ssm_conv_b[l], ssm_dt_bias[l],
                           ssm_a_log[l], ssm_d[l], ssm_norm_g[l])
        y_c = sliding_window_attention(q, k, v, attn_sinks[l])
        mix = jnp.concatenate([y_a.astype(h.dtype), y_b, y_c.astype(h.dtype)], axis=-1) @ w_out[l]
        h = layer_norm(DN_ALPHA * h + mix, ln1_g[l], ln1_b[l])
        ffn = peer_ffn(h, peer_wq[l], peer_keys[l], peer_u[l], peer_v[l])
        h = layer_norm(DN_ALPHA * h + ffn, ln2_g[l], ln2_b[l])
    return h
```

```python
import contextlib
import numpy as np
import concourse.bass as bass
import concourse.mybir as mybir
from concourse.bass_utils import run_bass_kernel_spmd

F32 = mybir.dt.float32
BF16 = mybir.dt.bfloat16
U32 = mybir.dt.uint32
AF = mybir.ActivationFunctionType
ALU = mybir.AluOpType
AX = mybir.AxisListType

D = 1024
INC = 4368
C_RGX, C_RGG, C_Z, C_XBC, C_DT, C_Q, C_K, C_V = 0, 512, 1024, 2048, 3584, 3600, 4112, 4240
ALPHA = 4.0 ** 0.25
EPS = 1e-5
EPOCH = 16000
NEG = -1.0e5


class Prog:
    ENG = ("pe", "dve", "act", "pool", "sp")

    def __init__(self, nc):
        self.nc = nc
        self.ops = []
        self.last_w = {}
        self.readers = {}
        self.dma_cum = {}
        self.bar = set()
        self.bar_seen = set()
        self.bank_w = {}
        self.bank_r = {}

    @staticmethod
    def bank_of(k):
        if isinstance(k, tuple):
            t = k[0]
            if t == "pb" or t == "pp":
                return k[1]
            if t == "pbz":
                return 2 + k[1]
            if t == "lg":
                return k[1]
            if t == "pv":
                return 4
            if t == "pt":
                return 7
            if t == "yoff":
                return 4 if k[1] == 0 else 6
            return None
        return {"pbv": 4, "pbdt": 4, "pcs": 6, "ptot": 6, "pb5": 5}.get(k)

    def op(self, eng, fn, reads=(), writes=(), dma=None, ndma=1):
        i = len(self.ops)
        deps = set()
        for k in reads:
            b = self.bank_of(k)
            if b is not None:
                if b in self.bank_w:
                    deps.add(self.bank_w[b])
                self.bank_r.setdefault(b, []).append(i)
        if eng == "pe":
            for k in writes:
                b = self.bank_of(k)
                if b is not None:
                    deps |= set(self.bank_r.get(b, ()))
                    self.bank_r[b] = []
                    self.bank_w[b] = i
        for k in reads:
            if k in self.last_w:
                deps.add(self.last_w[k])
        for k in writes:
            if k in self.last_w:
                deps.add(self.last_w[k])
            for r in self.readers.get(k, ()):
                deps.add(r)
        if self.bar and eng not in self.bar_seen:
            deps |= self.bar
            self.bar_seen.add(eng)
        deps.discard(i)
        if eng == "pe":
            deps = {d for d in deps if self.ops[d]["eng"] != "pe"}
        for k in writes:
            self.last_w[k] = i
            self.readers[k] = []
        for k in reads:
            self.readers.setdefault(k, []).append(i)
        o = dict(eng=eng, fn=fn, deps=deps, dma=dma, ndma=ndma, sig=False, cnt=None)
        if dma is not None:
            c = self.dma_cum.get(dma, 0) + 16 * ndma
            self.dma_cum[dma] = c
            o["cnt"] = c
            assert c < 60000, dma
        self.ops.append(o)
        return i

    def barrier(self):
        last = {}
        for i, o in enumerate(self.ops):
            if o["dma"] is not None:
                last[("d", o["dma"])] = i
            else:
                last[("c", o["eng"])] = i
        self.bar = set(last.values())
        self.bar_seen = set()

    def emit(self):
        nc = self.nc
        ops = self.ops
        for o in ops:
            for d in o["deps"]:
                ops[d]["sig"] = True
        cnt = {e: 0 for e in self.ENG}
        for o in ops:
            if o["dma"] is None and o["sig"]:
                cnt[o["eng"]] += 1
                o["cnt"] = cnt[o["eng"]]
        nep = {e: cnt[e] // EPOCH + 1 for e in self.ENG}
        with contextlib.ExitStack() as es:
            csem = {e: [es.enter_context(nc.semaphore(f"c_{e}_{j}")) for j in range(nep[e])] for e in self.ENG}
            dsem = {n: es.enter_context(nc.semaphore(f"d_{n}")) for n in self.dma_cum}
            block = es.enter_context(nc.Block())

            def semval(o):
                if o["dma"] is not None:
                    return dsem[o["dma"]], o["cnt"]
                c = o["cnt"]
                ep = (c - 1) // EPOCH
                return csem[o["eng"]][ep], c - ep * EPOCH

            def run(engname, eng):
                waited = {}
                for o in ops:
                    if o["eng"] != engname:
                        continue
                    need = {}
                    for d in o["deps"]:
                        s, v = semval(ops[d])
                        key = id(s)
                        if need.get(key, (None, 0))[1] < v:
                            need[key] = (s, v)
                    for key, (s, v) in need.items():
                        if waited.get(key, 0) >= v:
                            continue
                        waited[key] = v
                        eng.wait_ge(s, v)
                    ins = o["fn"](eng)
                    if o["dma"] is not None:
                        lst = ins if isinstance(ins, (list, tuple)) else [ins]
                        assert len(lst) == o["ndma"], (len(lst), o["ndma"])
                        s, _ = semval(o)
                        for x in lst:
                            x.then_inc(s, 16)
                    elif o["sig"]:
                        s, _ = semval(o)
                        ins.then_inc(s, 1)
                if engname == "sp":
                    for n, s in dsem.items():
                        eng.wait_ge(s, self.dma_cum[n])

            @block.sync
            def _(e):
                run("sp", e)

            @block.tensor
            def _(e):
                run("pe", e)

            @block.vector
            def _(e):
                run("dve", e)

            @block.scalar
            def _(e):
                run("act", e)

            @block.gpsimd
            def _(e):
                run("pool", e)


def bc(ap, shape):
    return ap.to_broadcast(list(shape))


def build_nc(NSEQ, S, DEPTH, TT=512, NEXP_BLK=128):
    nc = bass.Bass("TRN2", target_bir_lowering=False)
    NT = NSEQ * S
    NCH = S // 128
    P = Prog(nc)
    op = P.op

    def din(name, shape):
        return nc.dram_tensor(name, list(shape), F32, kind="ExternalInput").ap()

    x = din("x", [NT, D])
    emb_g, emb_b = din("emb_ln_g", [D]), din("emb_ln_b", [D])
    w_in = din("w_in", [DEPTH, D, INC])
    rg_conv_w, rg_conv_b = din("rg_conv_w", [DEPTH, 4, 512]), din("rg_conv_b", [DEPTH, 512])
    rg_wa, rg_ba = din("rg_wa", [DEPTH, 8, 64, 64]), din("rg_ba", [DEPTH, 512])
    rg_wx, rg_bx = din("rg_wx", [DEPTH, 8, 64, 64]), din("rg_bx", [DEPTH, 512])
    rg_lambda = din("rg_lambda", [DEPTH, 512])
    ssm_conv_w, ssm_conv_b = din("ssm_conv_w", [DEPTH, 4, 1536]), din("ssm_conv_b", [DEPTH, 1536])
    ssm_dt_bias, ssm_a_log, ssm_d = din("ssm_dt_bias", [DEPTH, 16]), din("ssm_a_log", [DEPTH, 16]), din("ssm_d", [DEPTH, 16])
    ssm_norm_g = din("ssm_norm_g", [DEPTH, D])
    attn_sinks = din("attn_sinks", [DEPTH, 8])
    w_out = din("w_out", [DEPTH, 2048, D])
    ln1_g, ln1_b = din("ln1_g", [DEPTH, D]), din("ln1_b", [DEPTH, D])
    peer_wq = din("peer_wq", [DEPTH, D, D])
    peer_keys = din("peer_keys", [DEPTH, 2, 128, 64])
    peer_u, peer_v = din("peer_u", [DEPTH, 16384, D]), din("peer_v", [DEPTH, 16384, D])
    ln2_g, ln2_b = din("ln2_g", [DEPTH, D]), din("ln2_b", [DEPTH, D])
    out = nc.dram_tensor("out", [NT, D], F32, kind="ExternalOutput").ap()
    hs0 = nc.dram_tensor("hs0", [NT, D], F32, kind="Internal").ap()
    hs1 = nc.dram_tensor("hs1", [NT, D], F32, kind="Internal").ap()
    NEB = NEXP_BLK
    UT = nc.dram_tensor("UTs", [DEPTH, NEB, 128, 8, 128], BF16, kind="Internal").ap()
    VB = nc.dram_tensor("VBs", [DEPTH, NEB, 128, D], BF16, kind="Internal").ap()

    es = contextlib.ExitStack()
    with es:
        def sb(name, shape, dt=F32, st=es):
            return st.enter_context(nc.sbuf_tensor(name, list(shape), dt))

        def ps(name, shape, dt=F32):
            return es.enter_context(nc.psum_tensor(name, list(shape), dt))

        PB = [ps(f"pb{i}", [128, 512]) for i in range(7)]
        PT = ps("pt", [128, 1024], BF16)
        ident = sb("ident", [128, 128], BF16)
        iof = sb("iof", [128, 128])
        iocol = sb("iocol", [128, 128])
        iocb = sb("iocb", [128, 128], BF16)
        triu = sb("triu", [128, 128])
        ones = sb("ones", [128, 128])
        smask = sb("smask", [128, 128])
        amask = sb("amask", [128, 2, 256])
        io16 = sb("io16", [128, 16])

        op("pool", lambda e: e.iota(iof[:], pattern=[[1, 128]], base=0, channel_multiplier=-1, allow_small_or_imprecise_dtypes=True), writes=["iof"])
        op("pool", lambda e: e.iota(iocol[:], pattern=[[1, 128]], base=0, channel_multiplier=0, allow_small_or_imprecise_dtypes=True), writes=["iocol"])
        op("pool", lambda e: e.iota(io16[:], pattern=[[1, 16]], base=0, channel_multiplier=0, allow_small_or_imprecise_dtypes=True), writes=["io16"])
        op("dve", lambda e: e.tensor_copy(out=iocb[:], in_=iocol[:]), reads=["iocol"], writes=["iocb"])
        op("dve", lambda e: e.tensor_single_scalar(out=ident[:], in_=iof[:], scalar=0.0, op=ALU.is_equal), reads=["iof"], writes=["ident"])
        op("dve", lambda e: e.tensor_single_scalar(out=triu[:], in_=iof[:], scalar=0.0, op=ALU.is_ge), reads=["iof"], writes=["triu"])
        op("dve", lambda e: e.memset(ones[:], 1.0), writes=["ones"])
        op("dve", lambda e: e.tensor_scalar(out=smask[:], in0=triu[:], scalar1=-1.0, scalar2=-NEG, op0=ALU.add, op1=ALU.mult), reads=["triu"], writes=["smask"])
        op("dve", lambda e: e.memset(amask[:, 0, 0:128], NEG), writes=["amask0a"])
        op("dve", lambda e: e.tensor_scalar(out=amask[:, 1, 0:128], in0=iof[:], scalar1=0.0, scalar2=NEG, op0=ALU.is_le, op1=ALU.mult), reads=["iof"], writes=["amask1a"])
        for v_ in range(2):
            op("dve", lambda e, v_=v_: e.tensor_scalar(out=amask[:, v_, 128:256], in0=iof[:], scalar1=0.0, scalar2=NEG, op0=ALU.is_gt, op1=ALU.mult), reads=["iof"], writes=[f"amask{v_}b"])
        AMK = ["amask0a", "amask1a", "amask0b", "amask1b"]

        def ln_rows(st, src, dst, gt, bt, gk, bk, tag, src_keys, dst_keys):
            stats, mv, rs = st["stats"], st["mv"], st["rs"]
            for j in range(2):
                op("dve", lambda e, j=j: e.bn_stats(out=stats[:, j, :], in_=src[:, j * 512:(j + 1) * 512]), reads=src_keys, writes=[f"stats{j}"])
            op("dve", lambda e: e.bn_aggr(out=mv[:], in_=stats[:]), reads=["stats0", "stats1"], writes=["mv"])
            op("act", lambda e: e.activation(out=rs[:, 0:1], in_=mv[:, 1:2], func=AF.Ln, bias=epsb[:, 0:1]), reads=["mv", "epsb"], writes=["rs0"])
            op("act", lambda e: e.activation(out=rs[:, 1:2], in_=rs[:, 0:1], func=AF.Exp, scale=-0.5), reads=["rs0"], writes=["rs1"])
            op("dve", lambda e: e.tensor_scalar(out=dst[:], in0=src[:], scalar1=mv[:, 0:1], scalar2=rs[:, 1:2], op0=ALU.subtract, op1=ALU.mult), reads=src_keys + ["mv", "rs1"], writes=dst_keys)
            op("pool", lambda e: e.tensor_tensor(out=dst[:], in0=dst[:], in1=gt[:], op=ALU.mult), reads=dst_keys + [gk], writes=dst_keys)
            op("pool", lambda e: e.tensor_tensor(out=dst[:], in0=dst[:], in1=bt[:], op=ALU.add), reads=dst_keys + [bk], writes=dst_keys)

        epsb = sb("epsb", [128, 1])
        op("dve", lambda e: e.memset(epsb[:], EPS), writes=["epsb"])
        lnst = dict(stats=sb("stats", [128, 2, 6]), mv=sb("mv", [128, 2]), rs=sb("rs", [128, 2]))

        with contextlib.ExitStack() as s0:
            g0 = sb("g0", [128, D], st=s0)
            b0 = sb("b0", [128, D], st=s0)
            xin = [sb(f"xin{i}", [128, D], st=s0) for i in range(2)]
            xo = [sb(f"xo{i}", [128, D], st=s0) for i in range(2)]
            op("sp", lambda e: e.dma_start(out=g0[:], in_=emb_g.partition_broadcast(128)), writes=["g0"], dma="c_g0")
            op("sp", lambda e: e.dma_start(out=b0[:], in_=emb_b.partition_broadcast(128)), writes=["b0"], dma="c_b0")
            for c in range(NT // 128):
                i = c % 2
                op("sp", lambda e, c=c, i=i: e.dma_start(out=xin[i][:], in_=x[c * 128:(c + 1) * 128, :]), writes=[f"xin{i}"], dma=f"xin{i}")
                ln_rows(lnst, xin[i], xo[i], g0, b0, "g0", "b0", "e", [f"xin{i}"], [f"xo{i}"])
                op("sp", lambda e, c=c, i=i: e.dma_start(out=hs0[c * 128:(c + 1) * 128, :], in_=xo[i][:]), reads=[f"xo{i}"], writes=[("hs0", c)], dma=f"xo{i}")
            ust = [sb(f"ust{i}", [128, D], st=s0) for i in range(2)]
            ubf = [sb(f"ubf{i}", [128, D], BF16, st=s0) for i in range(2)]
            utb = [sb(f"utb{i}", [128, 8, 128], BF16, st=s0) for i in range(2)]
            vst = [sb(f"vst{i}", [128, D], st=s0) for i in range(2)]
            vbf = [sb(f"vbf{i}", [128, D], BF16, st=s0) for i in range(2)]
            n = 0
            for l in range(DEPTH):
                for eb in range(NEB):
                    i = n % 2
                    n += 1
                    op("sp", lambda e, l=l, eb=eb, i=i: e.dma_start(out=ust[i][:], in_=peer_u[l, eb * 128:(eb + 1) * 128, :]), writes=[f"ust{i}"], dma=f"ust{i}")
                    op("sp", lambda e, l=l, eb=eb, i=i: e.dma_start(out=vst[i][:], in_=peer_v[l, eb * 128:(eb + 1) * 128, :]), writes=[f"vst{i}"], dma=f"vst{i}")
                    op("act", lambda e, i=i: e.copy(out=ubf[i][:], in_=ust[i][:]), reads=[f"ust{i}"], writes=[f"ubf{i}"])
                    op("pool", lambda e, i=i: e.tensor_copy(out=vbf[i][:], in_=vst[i][:]), reads=[f"vst{i}"], writes=[f"vbf{i}"])
                    for kc in range(8):
                        op("pe", lambda e, i=i, kc=kc: e.transpose(out=PT[:, kc * 128:(kc + 1) * 128], in_=ubf[i][:, kc * 128:(kc + 1) * 128], identity=ident[:]),
                           reads=[f"ubf{i}", "ident"], writes=[("pt", kc)])
                    op("dve", lambda e, i=i: e.tensor_copy(out=utb[i][:], in_=PT[:].rearrange("p (k e) -> p k e", k=8)), reads=[("pt", kc) for kc in range(8)], writes=[f"utb{i}"])
                    op("sp", lambda e, l=l, eb=eb, i=i: e.dma_start(out=UT[l, eb], in_=utb[i][:]), reads=[f"utb{i}"], writes=[("UT", l, eb)], dma=f"utb{i}")
                    op("sp", lambda e, l=l, eb=eb, i=i: e.dma_start(out=VB[l, eb], in_=vbf[i][:]), reads=[f"vbf{i}"], writes=[("VB", l, eb)], dma=f"vbf{i}")
        P.barrier()

        import os
        dbg = os.environ.get("K_DBG", "")
        for l in range(DEPTH):
            if dbg in ("", "mix", "mixpeer"):
                mixer_phase(nc, P, es, locals(), l)
                P.barrier()
            if dbg in ("", "mixpeer"):
                peer_phase(nc, P, es, locals(), l, last=(l == DEPTH - 1))
                P.barrier()
        P.emit()
    return nc


def mixer_phase(nc, P, es_outer, E, l):
    op = P.op
    g = E
    NSEQ, S, NCH = g["NSEQ"], g["S"], g["NCH"]
    PB, PT = g["PB"], g["PT"]
    ident, iof, triu, ones, smask, amask, AMK = g["ident"], g["iof"], g["triu"], g["ones"], g["smask"], g["amask"], g["AMK"]
    hs0, hs1 = g["hs0"], g["hs1"]
    lnst, ln_rows, epsb = g["lnst"], g["ln_rows"], g["epsb"]
    with contextlib.ExitStack() as st:
        def sb(name, shape, dt=F32):
            return st.enter_context(nc.sbuf_tensor(f"{name}_{l}", list(shape), dt))
        Wi = sb("Wi", [128, 8, INC], BF16)
        Wo = sb("Wo", [128, 16, D], BF16)
        Wg = sb("Wg", [128, 4, 2, 128], BF16)
        stg0_ = sb("stg0", [128, 2184]); stg = [stg0_, stg0_]
        l1g, l1b, ng = sb("l1g", [128, D]), sb("l1b", [128, D]), sb("ng", [128, D])
        cw = sb("cw", [128, 16, 5])
        rgc = sb("rgc", [128, 4, 4])
        dtb = sb("dtb", [128, 16]); aneg = sb("aneg", [128, 16]); dsk = sb("dsk", [128, 16]); snk = sb("snk", [128, 8])
        wv = g["w_in"][l].rearrange("(kc p) n -> p kc n", p=128)
        n = 0
        for kc in range(8):
            for hf in range(2):
                i = 0; n += 1
                c0 = hf * 2184
                op("sp", lambda e, kc=kc, c0=c0, i=i: e.dma_start(out=stg[i][:], in_=wv[:, kc, c0:c0 + 2184]), writes=[f"stg{i}"], dma=f"stg{i}")
                if hf == 0:
                    op("act", lambda e, kc=kc, i=i: e.copy(out=Wi[:, kc, 0:2184], in_=stg[i][:]), reads=[f"stg{i}"], writes=[("Wi", kc, 0)])
                else:
                    op("act", lambda e, kc=kc, i=i: e.copy(out=Wi[:, kc, 2184:C_Q], in_=stg[i][:, 0:C_Q - 2184]), reads=[f"stg{i}"], writes=[("Wi", kc, 1)])
                    op("dve", lambda e, kc=kc, i=i: e.tensor_copy(
                        out=Wi[:, kc, C_Q:C_K].rearrange("p (r kv d) -> p kv r d", r=4, kv=2, d=64),
                        in_=stg[i][:, C_Q - 2184:C_K - 2184].rearrange("p (kv r d) -> p kv r d", r=4, kv=2, d=64)), reads=[f"stg{i}"], writes=[("Wi", kc, 2)])
                    op("act", lambda e, kc=kc, i=i: e.copy(out=Wi[:, kc, C_K:INC], in_=stg[i][:, C_K - 2184:INC - 2184]), reads=[f"stg{i}"], writes=[("Wi", kc, 3)])
        WIK = [("Wi", kc, j) for kc in range(8) for j in range(4)]
        wov = g["w_out"][l].rearrange("(kc p) n -> p kc n", p=128)
        for kc in range(16):
            i = 0; n += 1
            op("sp", lambda e, kc=kc, i=i: e.dma_start(out=stg[i][:, 0:D], in_=wov[:, kc, :]), writes=[f"stg{i}"], dma=f"stg{i}")
            op("act", lambda e, kc=kc, i=i: e.copy(out=Wo[:, kc, :], in_=stg[i][:, 0:D]), reads=[f"stg{i}"], writes=[("Wo", kc)])
        WOK = [("Wo", kc) for kc in range(16)]
        op("dve", lambda e: e.memset(wgs[:], 0.0), writes=["wgs"])
        for gi, wsrc in enumerate((g["rg_wa"], g["rg_wx"])):
            for hf in range(2):
                srcv = wsrc[l].rearrange("(b two) c d -> two c b d", two=2)[hf]
                op("sp", lambda e, gi=gi, hf=hf, srcv=srcv: e.dma_start(out=wgs[hf * 64:(hf + 1) * 64, :, gi, hf * 64:(hf + 1) * 64], in_=srcv), reads=[], writes=["wgs"], dma="c_wgs")
        op("dve", lambda e: e.tensor_copy(out=Wg[:], in_=wgs[:]), reads=["wgs"], writes=["Wg"])
        for t_, src_, k_ in ((l1g, g["ln1_g"], "l1g"), (l1b, g["ln1_b"], "l1b"), (ng, g["ssm_norm_g"], "ng")):
            op("sp", lambda e, t_=t_, src_=src_: e.dma_start(out=t_[:], in_=src_[l].partition_broadcast(128)), writes=[k_], dma="c_" + k_)
        for t_, src_, k_ in ((dtb, g["ssm_dt_bias"], "dtb"), (aneg, g["ssm_a_log"], "aneg"), (dsk, g["ssm_d"], "dsk"), (snk, g["attn_sinks"], "snk")):
            op("sp", lambda e, t_=t_, src_=src_: e.dma_start(out=t_[:], in_=src_[l].partition_broadcast(128)), writes=[k_], dma="c_" + k_)
        for k_ in range(4):
            op("sp", lambda e, k_=k_: e.dma_start(out=cw[:, 0:4, k_:k_ + 1], in_=g["rg_conv_w"][l, k_].rearrange("(b p o) -> p b o", p=128, o=1), allow_slow_non_contiguous=True), writes=["cw"], dma="c_cw")
            op("sp", lambda e, k_=k_: e.dma_start(out=cw[:, 4:16, k_:k_ + 1], in_=g["ssm_conv_w"][l, k_].rearrange("(b p o) -> p b o", p=128, o=1), allow_slow_non_contiguous=True), writes=["cw"], dma="c_cw")
        op("sp", lambda e: e.dma_start(out=cw[:, 0:4, 4:5], in_=g["rg_conv_b"][l].rearrange("(b p o) -> p b o", p=128, o=1), allow_slow_non_contiguous=True), writes=["cw"], dma="c_cw")
        op("sp", lambda e: e.dma_start(out=cw[:, 4:16, 4:5], in_=g["ssm_conv_b"][l].rearrange("(b p o) -> p b o", p=128, o=1), allow_slow_non_contiguous=True), writes=["cw"], dma="c_cw")
        for j, src_ in enumerate((g["rg_ba"], g["rg_bx"], g["rg_lambda"])):
            op("sp", lambda e, j=j, src_=src_: e.dma_start(out=rgc[:, :, j:j + 1], in_=src_[l].rearrange("(b p o) -> p b o", p=128, o=1), allow_slow_non_contiguous=True), writes=["rgc"], dma="c_rgc")
        op("dve", lambda e: e.tensor_scalar(out=cw[:, 4:16, :], in0=cw[:, 4:16, :], scalar1=0.5, scalar2=None, op0=ALU.mult), reads=["cw"], writes=["cw"])
        op("act", lambda e: e.activation(out=rgc[:, :, 3], in_=rgc[:, :, 2], func=AF.Exp, scale=-1.0), reads=["rgc"], writes=["rgc3"])
        op("act", lambda e: e.activation(out=rgc[:, :, 3], in_=rgc[:, :, 3], func=AF.Ln, bias=1.0), reads=["rgc3"], writes=["rgc3"])
        op("dve", lambda e: e.tensor_scalar(out=rgc[:, :, 2], in0=rgc[:, :, 3], scalar1=-4.0, scalar2=None, op0=ALU.mult), reads=["rgc3", "rgc"], writes=["rgc"])
        op("dve", lambda e: e.tensor_scalar(out=rgc[:, :, 0:2], in0=rgc[:, :, 0:2], scalar1=0.5, scalar2=None, op0=ALU.mult), reads=["rgc"], writes=["rgc"])
        op("act", lambda e: e.activation(out=aneg[:], in_=aneg[:], func=AF.Exp), reads=["aneg"], writes=["aneg"])
        op("dve", lambda e: e.tensor_scalar(out=aneg[:], in0=aneg[:], scalar1=-1.0, scalar2=None, op0=ALU.mult), reads=["aneg"], writes=["aneg"])
        op("pool", lambda e: e.tensor_scalar(out=ng[:], in0=ng[:], scalar1=0.5, scalar2=None, op0=ALU.mult), reads=["ng"], writes=["ng"])
        h = sb("h", [128, D]); hb = sb("hb", [128, D], BF16); yb = hb; hT = sb("hT", [128, 8, 128], BF16)
        X = sb("X", [128, 16, 131]); xc = sb("xc", [128, 16, 128]); wgs = xc[:, 0:8, :].rearrange("p (a b) t -> p a b t", a=4)
        gate = sb("gate", [128, 4, 128]); qT = sb("qT", [128, 4, 128], BF16)
        kT = sb("kT", [128, 2, 128], BF16); vr = sb("vr", [128, 2, 128], BF16)
        gz = sb("gz", [128, D]); dt = sb("dt", [128, 16]); sm = sb("sm", [128, 16, 8])
        ymT = sb("ymT", [128, 16, 128], BF16)
        lm = sb("lm", [128, 256]); pb = sb("pb", [128, 256], BF16); pTt = sb("pTt", [128, 2, 128], BF16)
        yc = sb("yc", [128, 512], BF16)
        T1KEYS = ["tA", "tB", "tC", "tD", "tE", "tF", "tG"]
        xcb = sb("xcb", [128, 128], BF16); lru = sb("lru", [128, 4, 128]); lst = sb("lst", [128, 4])
        xsb = sb("xsb", [128, 12, 128], BF16)
        xtm = sb("xtm", [128, D], BF16); xdt = sb("xdt", [128, D], BF16); xds = sb("xds", [128, D], BF16); btm = sb("btm", [128, 2, 128], BF16)
        daU = sb("daU", [128, 8, 128]); seg = sb("seg", [128, 8, 128]); t1 = [seg[:, i, :] for i in range(8)]; MT = sb("MT", [128, 8, 128], BF16)
        ssf = sb("ssf", [128, D]); ssb = sb("ssb", [128, D], BF16)
        y1 = sb("y1", [128, D]); y2 = sb("y2", [128, D])
        r1 = y2; ho = y1; th = sb("th", [128, 128]); rn = sb("rn", [128, 4])
        s16 = lambda i: sm[:, :, i]
        PTv = PT[:].rearrange("p (k e) -> p k e", k=8)

        for sq in range(NSEQ):
            op("dve", lambda e: e.memset(X[:, :, 128:131], 0.0), writes=["Xhist"])
            op("dve", lambda e: e.memset(lst[:], 0.0), writes=["lst"])
            op("dve", lambda e: e.memset(ssf[:], 0.0), writes=["ssf"])
            op("dve", lambda e: e.memset(ssb[:], 0.0), writes=["ssb"])
            op("dve", lambda e: e.memset(kT[:, 1, :], 0.0), writes=["kT1"])
            op("dve", lambda e: e.memset(vr[:, 1, :], 0.0), writes=["vr1"])
            for c in range(NCH):
                ci = sq * NCH + c
                rows = slice(ci * 128, (ci + 1) * 128)
                slot, pslot = c % 2, (c + 1) % 2
                op("sp", lambda e, rows=rows: e.dma_start(out=h[:], in_=hs0[rows, :]), reads=[("hs0", ci)], writes=["h"], dma="h")
                op("act", lambda e: e.copy(out=hb[:], in_=h[:]), reads=["h"], writes=["hb", ("yb", 0), ("yb", 1)])
                for kc in range(8):
                    op("pe", lambda e, kc=kc: e.transpose(out=PT[:, kc * 128:(kc + 1) * 128], in_=hb[:, kc * 128:(kc + 1) * 128], identity=ident[:]), reads=["hb", "ident"], writes=[("pt", kc)])
                op("dve", lambda e: e.tensor_copy(out=hT[:], in_=PTv), reads=[("pt", k) for k in range(8)], writes=["hT"])
                op("pool", lambda e: e.tensor_copy(out=X[:, :, 0:3], in_=X[:, :, 128:131]), reads=["Xhist", "X"], writes=["Xh0"])

                def fm_group(cols, bank, nblk):
                    for b, c0 in enumerate(cols):
                        for kc in range(8):
                            op("pe", lambda e, b=b, c0=c0, kc=kc: e.matmul(PB[bank][:, b * 128:(b + 1) * 128], lhsT=Wi[:, kc, c0:c0 + 128], rhs=hT[:, kc, :], start=(kc == 0), stop=(kc == 7)),
                               reads=["hT"] + WIK, writes=[("pb", bank, b)])
                    return [("pb", bank, b) for b in range(nblk)]
                ks = fm_group([C_RGX + 128 * b for b in range(4)], 0, 4)
                op("act", lambda e: e.copy(out=X[:, 0:4, 3:131], in_=PB[0][:].rearrange("p (b t) -> p b t", b=4)), reads=ks + ["Xh0"], writes=["X", "Xhist"])
                ks = fm_group([C_RGG + 128 * b for b in range(4)], 1, 4)
                op("act", lambda e: e.copy(out=gate[:], in_=PB[1][:].rearrange("p (b t) -> p b t", b=4)), reads=ks, writes=["gate"])
                for gq in range(3):
                    ks = fm_group([C_XBC + 128 * (4 * gq + b) for b in range(4)], gq % 2, 4)
                    op("act", lambda e, gq=gq: e.copy(out=X[:, 4 + 4 * gq:8 + 4 * gq, 3:131], in_=PB[gq % 2][:].rearrange("p (b t) -> p b t", b=4)), reads=ks + ["Xh0"], writes=["X", "Xhist"])
                ks = fm_group([C_Q + 128 * b for b in range(4)], 1, 4)
                op("act", lambda e: e.copy(out=qT[:], in_=PB[1][:].rearrange("p (b t) -> p b t", b=4)), reads=ks, writes=["qT"])
                ks = fm_group([C_K], 0, 1)
                op("act", lambda e, slot=slot: e.copy(out=kT[:, slot, :], in_=PB[0][:, 0:128]), reads=ks, writes=[f"kT{slot}"])
                for hf in range(2):
                    for kc in range(8):
                        op("pe", lambda e, hf=hf, kc=kc: e.matmul(PB[2 + hf][:, :], lhsT=hT[:, kc, :], rhs=Wi[:, kc, C_Z + 512 * hf:C_Z + 512 * (hf + 1)], start=(kc == 0), stop=(kc == 7)),
                           reads=["hT"] + WIK, writes=[("pbz", hf)])
                for kc in range(8):
                    op("pe", lambda e, kc=kc: e.matmul(PB[4][:, 0:128], lhsT=hT[:, kc, :], rhs=Wi[:, kc, C_V:C_V + 128], start=(kc == 0), stop=(kc == 7)), reads=["hT"] + WIK, writes=["pbv"])
                for kc in range(8):
                    op("pe", lambda e, kc=kc: e.matmul(PB[4][:, 128:144], lhsT=hT[:, kc, :], rhs=Wi[:, kc, C_DT:C_DT + 16], start=(kc == 0), stop=(kc == 7)), reads=["hT"] + WIK, writes=["pbdt"])
                op("act", lambda e, slot=slot: e.copy(out=vr[:, slot, :], in_=PB[4][:, 0:128]), reads=["pbv"], writes=[f"vr{slot}"])
                for hf in range(2):
                    op("act", lambda e, hf=hf: e.activation(out=gz[:, 512 * hf:512 * (hf + 1)], in_=PB[2 + hf][:, :], func=AF.Tanh, scale=0.5), reads=[("pbz", hf)], writes=[("gz", hf)])
                    op("dve", lambda e, hf=hf: e.scalar_tensor_tensor(out=gz[:, 512 * hf:512 * (hf + 1)], in0=gz[:, 512 * hf:512 * (hf + 1)], scalar=1.0, in1=PB[2 + hf][:, :], op0=ALU.add, op1=ALU.mult),
                       reads=[("gz", hf), ("pbz", hf)], writes=[("gz", hf)])
                op("dve", lambda e: e.tensor_tensor(out=s16(0), in0=PB[4][:, 128:144], in1=dtb[:], op=ALU.add), reads=["pbdt", "dtb"], writes=["s0"])
                op("act", lambda e: e.activation(out=s16(0), in_=s16(0), func=AF.Exp), reads=["s0"], writes=["s0"])
                op("act", lambda e: e.activation(out=dt[:], in_=s16(0), func=AF.Ln, bias=1.0), reads=["s0"], writes=["dt"])
                op("dve", lambda e: e.tensor_tensor(out=s16(1), in0=dt[:], in1=aneg[:], op=ALU.mult), reads=["dt", "aneg"], writes=["da"])

                for hd in range(8):
                    r_, kv = hd % 4, hd // 4
                    pr = slice(kv * 64, kv * 64 + 64)
                    bk = 5 + hd % 2
                    op("pe", lambda e, r_=r_, pr=pr, bk=bk, pslot=pslot: e.matmul(PB[bk][:, 0:128], lhsT=qT[pr, r_, :], rhs=kT[pr, pslot, :], start=True, stop=True), reads=["qT", f"kT{pslot}"], writes=[("lg", bk, 0)])
                    op("pe", lambda e, r_=r_, pr=pr, bk=bk, slot=slot: e.matmul(PB[bk][:, 128:256], lhsT=qT[pr, r_, :], rhs=kT[pr, slot, :], start=True, stop=True), reads=["qT", f"kT{slot}"], writes=[("lg", bk, 1)])
                    mv_ = 0 if c == 0 else 1
                    op("dve", lambda e, bk=bk, mv_=mv_: e.tensor_tensor(out=lm[:], in0=PB[bk][:, 0:256], in1=amask[:, mv_, :], op=ALU.add), reads=[("lg", bk, 0), ("lg", bk, 1)] + AMK, writes=["lm"])
                    op("dve", lambda e: e.reduce_max(out=s16(2)[:, 0:1], in_=lm[:], axis=AX.X), reads=["lm"], writes=["amx"])
                    op("dve", lambda e, hd=hd: e.tensor_scalar(out=s16(2)[:, 1:2], in0=s16(2)[:, 0:1], scalar1=0.125, scalar2=snk[:, hd:hd + 1], op0=ALU.mult, op1=ALU.max), reads=["amx", "snk"], writes=["am"])
                    op("dve", lambda e: e.tensor_scalar(out=s16(2)[:, 2:3], in0=s16(2)[:, 1:2], scalar1=-1.0, scalar2=None, op0=ALU.mult), reads=["am"], writes=["anm"])
                    op("dve", lambda e: e.memset(s16(2)[:, 3:4], 0.0), writes=["asum"])
                    op("act", lambda e: e.activation(out=pb[:], in_=lm[:], func=AF.Exp, bias=s16(2)[:, 2:3], scale=0.125, accum_out=s16(2)[:, 3:4]), reads=["lm", "anm", "asum"], writes=["pb", "asum"])
                    op("act", lambda e, hd=hd: e.activation(out=s16(2)[:, 4:5], in_=snk[:, hd:hd + 1], func=AF.Exp, bias=s16(2)[:, 2:3]), reads=["snk", "anm"], writes=["asnk"])
                    op("dve", lambda e: e.tensor_tensor(out=s16(2)[:, 5:6], in0=s16(2)[:, 3:4], in1=s16(2)[:, 4:5], op=ALU.add), reads=["asum", "asnk"], writes=["aden"])
                    op("dve", lambda e: e.reciprocal(out=s16(2)[:, 6:7], in_=s16(2)[:, 5:6]), reads=["aden"], writes=["arec"])
                    for j in range(2):
                        op("pe", lambda e, j=j: e.transpose(out=PT[:, j * 128:(j + 1) * 128], in_=pb[:, j * 128:(j + 1) * 128], identity=ident[:]), reads=["pb", "ident"], writes=[("pt", j)])
                    op("act", lambda e: e.copy(out=pTt[:], in_=PTv[:, 0:2, :]), reads=[("pt", 0), ("pt", 1)], writes=["pTt"])
                    osl = slice(hd * 64, hd * 64 + 64)
                    vs = slice(kv * 64, kv * 64 + 64)
                    op("pe", lambda e, osl=osl, vs=vs, pslot=pslot: e.matmul(PB[4][:, osl], lhsT=pTt[:, 0, :], rhs=vr[:, pslot, vs], start=True, stop=False), reads=["pTt", f"vr{pslot}"], writes=[("pv", hd)])
                    op("pe", lambda e, osl=osl, vs=vs, slot=slot: e.matmul(PB[4][:, osl], lhsT=pTt[:, 1, :], rhs=vr[:, slot, vs], start=False, stop=True), reads=["pTt", f"vr{slot}", ("pv", hd)], writes=[("pv", hd)])
                    op("dve", lambda e, osl=osl: e.tensor_scalar(out=yc[:, osl], in0=PB[4][:, osl], scalar1=s16(2)[:, 6:7], scalar2=None, op0=ALU.mult), reads=[("pv", hd), "arec", "pbdt", "pbv"], writes=[("yc", hd)])
                for j in range(4):
                    op("pe", lambda e, j=j: e.transpose(out=PT[:, j * 128:(j + 1) * 128], in_=yc[:, j * 128:(j + 1) * 128], identity=ident[:]), reads=[("yc", hd) for hd in range(8)] + ["ident"], writes=[("pt", j)])
                op("act", lambda e: e.copy(out=ymT[:, 12:16, :], in_=PTv[:, 0:4, :]), reads=[("pt", j) for j in range(4)], writes=["ymT_c"])
                if c == 0:
                    pass

                for b in range(16):
                    eng = "dve"
                    op(eng, lambda e, b=b: e.tensor_scalar(out=xc[:, b, :], in0=X[:, b, 0:128], scalar1=cw[:, b, 0:1], scalar2=cw[:, b, 4:5], op0=ALU.mult, op1=ALU.add), reads=["X", "Xh0", "cw"], writes=[("xc", b), "wgs"])
                    for k in range(1, 4):
                        op(eng, lambda e, b=b, k=k: e.scalar_tensor_tensor(out=xc[:, b, :], in0=X[:, b, k:k + 128], scalar=cw[:, b, k:k + 1], in1=xc[:, b, :], op0=ALU.mult, op1=ALU.add), reads=["X", "Xh0", "cw", ("xc", b)], writes=[("xc", b)])

                for b in range(4):
                    A, B_, C_, D_, E_, F_, G_, H_ = [t1[i] for i in range(8)]
                    op("act", lambda e, b=b: e.copy(out=xcb[:], in_=xc[:, b, :]), reads=[("xc", b)], writes=["xcb"])
                    bk = 5 + b % 2
                    for gi in range(2):
                        op("pe", lambda e, b=b, gi=gi, bk=bk: e.matmul(PB[bk][:, gi * 128:(gi + 1) * 128], lhsT=Wg[:, b, gi, :], rhs=xcb[:], start=True, stop=True), reads=["Wg", "xcb"], writes=[("lg", bk, gi)])
                    op("act", lambda e, b=b, bk=bk: e.activation(out=A[:], in_=PB[bk][:, 0:128], func=AF.Tanh, bias=rgc[:, b, 0:1], scale=0.5), reads=[("lg", bk, 0), "rgc"], writes=["tA", ("seg", 0), ("seg", 1)])
                    op("act", lambda e, b=b, bk=bk: e.activation(out=B_[:], in_=PB[bk][:, 128:256], func=AF.Tanh, bias=rgc[:, b, 1:2], scale=0.5), reads=[("lg", bk, 1), "rgc"], writes=["tB", ("seg", 0), ("seg", 1)])
                    op("dve", lambda e, b=b: e.tensor_scalar(out=C_[:], in0=A[:], scalar1=1.0, scalar2=rgc[:, b, 2:3], op0=ALU.add, op1=ALU.mult), reads=["tA", "rgc"], writes=["tC"])
                    op("act", lambda e: e.activation(out=D_[:], in_=C_[:], func=AF.Exp), reads=["tC"], writes=["tD"])
                    op("dve", lambda e: e.tensor_scalar(out=E_[:], in0=C_[:], scalar1=2.0 / 6.0, scalar2=1.0, op0=ALU.mult, op1=ALU.add), reads=["tC"], writes=["tE"])
                    for dv in (5.0, 4.0, 3.0, 2.0):
                        op("dve", lambda e, dv=dv: e.scalar_tensor_tensor(out=E_[:], in0=C_[:], scalar=2.0 / dv, in1=E_[:], op0=ALU.mult, op1=ALU.mult), reads=["tC", "tE"], writes=["tE"])
                        op("dve", lambda e: e.tensor_scalar(out=E_[:], in0=E_[:], scalar1=1.0, scalar2=None, op0=ALU.add), reads=["tE"], writes=["tE"])
                    op("dve", lambda e: e.scalar_tensor_tensor(out=E_[:], in0=C_[:], scalar=-2.0, in1=E_[:], op0=ALU.mult, op1=ALU.mult), reads=["tC", "tE"], writes=["tE"])
                    op("dve", lambda e: e.tensor_scalar(out=E_[:], in0=E_[:], scalar1=1e-30, scalar2=None, op0=ALU.max), reads=["tE"], writes=["tE"])
                    op("act", lambda e: e.activation(out=E_[:], in_=E_[:], func=AF.Ln), reads=["tE"], writes=["tE"])
                    op("act", lambda e: e.activation(out=E_[:], in_=E_[:], func=AF.Exp, scale=0.5), reads=["tE"], writes=["tE"])
                    op("dve", lambda e, b=b: e.scalar_tensor_tensor(out=F_[:], in0=B_[:], scalar=1.0, in1=xc[:, b, :], op0=ALU.add, op1=ALU.mult), reads=["tB", ("xc", b)], writes=["tF"])
                    op("dve", lambda e: e.scalar_tensor_tensor(out=F_[:], in0=F_[:], scalar=0.5, in1=E_[:], op0=ALU.mult, op1=ALU.mult), reads=["tF", "tE"], writes=["tF"])
                    op("dve", lambda e, b=b: e.tensor_tensor_scan(out=lru[:, b, :], data0=D_[:], data1=F_[:], initial=lst[:, b:b + 1], op0=ALU.mult, op1=ALU.add), reads=["tD", "tF", "lst"], writes=[("lru", b)])
                    op("dve", lambda e, b=b: e.tensor_copy(out=lst[:, b:b + 1], in_=lru[:, b, 127:128]), reads=[("lru", b)], writes=["lst"])
                    op("pool", lambda e, b=b: e.tensor_tensor(out=G_[:], in0=gate[:, b, :], in1=gate[:, b, :], op=ALU.mult), reads=["gate"], writes=["tG"])
                    op("pool", lambda e: e.tensor_scalar(out=G_[:], in0=G_[:], scalar1=0.044715, scalar2=1.0, op0=ALU.mult, op1=ALU.add), reads=["tG"], writes=["tG"])
                    op("pool", lambda e, b=b: e.tensor_tensor(out=G_[:], in0=G_[:], in1=gate[:, b, :], op=ALU.mult), reads=["tG", "gate"], writes=["tG"])
                    op("act", lambda e: e.activation(out=G_[:], in_=G_[:], func=AF.Tanh, scale=0.7978845608028654), reads=["tG"], writes=["tG"])
                    op("dve", lambda e, b=b: e.scalar_tensor_tensor(out=G_[:], in0=G_[:], scalar=1.0, in1=gate[:, b, :], op0=ALU.add, op1=ALU.mult), reads=["tG", "gate"], writes=["tG"])
                    op("dve", lambda e, b=b: e.scalar_tensor_tensor(out=ymT[:, b, :], in0=G_[:], scalar=0.5, in1=lru[:, b, :], op0=ALU.mult, op1=ALU.mult), reads=["tG", ("lru", b)], writes=[("ymT_a", b)])

                for b in range(12):
                    op("act", lambda e, b=b: e.activation(out=th[:], in_=xc[:, 4 + b, :], func=AF.Tanh), reads=[("xc", 4 + b)], writes=["th"])
                    op("dve", lambda e, b=b: e.scalar_tensor_tensor(out=xsb[:, b, :], in0=th[:], scalar=1.0, in1=xc[:, 4 + b, :], op0=ALU.add, op1=ALU.mult), reads=["th", ("xc", 4 + b)], writes=[("xsb", b)])
                for kc in range(8):
                    op("pe", lambda e, kc=kc: e.transpose(out=PT[:, kc * 128:(kc + 1) * 128], in_=xsb[:, kc, :], identity=ident[:]), reads=[("xsb", kc), "ident"], writes=[("pt", kc)])
                op("act", lambda e: e.copy(out=xtm[:], in_=PT[:]), reads=[("pt", k) for k in range(8)], writes=["xtm"])
                for j in range(2):
                    op("pe", lambda e, j=j: e.transpose(out=PT[:, j * 128:(j + 1) * 128], in_=xsb[:, 8 + j, :], identity=ident[:]), reads=[("xsb", 8 + j), "ident"], writes=[("pt", j)])
                op("act", lambda e: e.copy(out=btm[:], in_=PTv[:, 0:2, :]), reads=[("pt", 0), ("pt", 1)], writes=["btm"])
                op("pe", lambda e: e.matmul(PB[6][:, 0:16], lhsT=triu[:], rhs=s16(1), start=True, stop=True), reads=["triu", "da"], writes=["pcs"])
                op("pe", lambda e: e.matmul(PB[6][:, 16:32], lhsT=ones[:], rhs=s16(1), start=True, stop=True), reads=["ones", "da"], writes=["ptot"])
                op("dve", lambda e: e.tensor_copy(out=s16(3), in_=PB[6][:, 0:16]), reads=["pcs"], writes=["cs"])
                op("dve", lambda e: e.tensor_copy(out=s16(4), in_=PB[6][:, 16:32]), reads=["ptot"], writes=["tot"])
                op("act", lambda e: e.activation(out=s16(5), in_=s16(3), func=AF.Exp), reads=["cs"], writes=["ea"])
                op("dve", lambda e: e.tensor_tensor(out=s16(6), in0=s16(4), in1=s16(3), op=ALU.subtract), reads=["cs", "tot"], writes=["ds"])
                op("act", lambda e: e.activation(out=s16(6), in_=s16(6), func=AF.Exp), reads=["ds"], writes=["ds"])
                op("act", lambda e: e.activation(out=s16(7), in_=s16(4), func=AF.Exp), reads=["tot"], writes=["cd"])
                op("dve", lambda e: e.tensor_tensor(out=s16(6), in0=s16(6), in1=dt[:], op=ALU.mult), reads=["ds", "dt"], writes=["ds"])
                x3 = xtm[:].rearrange("p (h d) -> p h d", h=16)
                op("dve", lambda e: e.tensor_tensor(out=xdt[:].rearrange("p (h d) -> p h d", h=16), in0=x3, in1=bc(dt[:].unsqueeze(2), [128, 16, 64]), op=ALU.mult), reads=["xtm", "dt"], writes=["xdt"])
                op("pool", lambda e: e.tensor_tensor(out=xds[:].rearrange("p (h d) -> p h d", h=16), in0=x3, in1=bc(s16(6).unsqueeze(2), [128, 16, 64]), op=ALU.mult), reads=["xtm", "ds"], writes=["xds"])
                for g_ in range(2):
                    hs_ = slice(8 * g_, 8 * g_ + 8)
                    cols = slice(512 * g_, 512 * g_ + 512)
                    op("dve", lambda e, hs_=hs_: e.tensor_tensor(out=daU[:], in0=bc(triu[:].unsqueeze(1), [128, 8, 128]), in1=bc(s16(1)[:, hs_].unsqueeze(2), [128, 8, 128]), op=ALU.mult), reads=["triu", "da"], writes=["daU"])
                    for j in range(2):
                        op("pe", lambda e, j=j: e.matmul(PB[2 + j][:, :], lhsT=ones[:], rhs=daU[:, 4 * j:4 * j + 4, :].rearrange("p a b -> p (a b)"), start=True, stop=True), reads=["ones", "daU"], writes=[("pbz", j)])
                        op("dve", lambda e, j=j, g_=g_: e.tensor_tensor(out=seg[:, 4 * j:4 * j + 4, :], in0=PB[2 + j][:, :].rearrange("p (a b) -> p a b", a=4),
                                                                  in1=bc(s16(3)[:, 8 * g_ + 4 * j:8 * g_ + 4 * j + 4].unsqueeze(2), [128, 4, 128]), op=ALU.subtract), reads=[("pbz", j), "cs"], writes=[("seg", j)] + T1KEYS)
                    op("pool", lambda e: e.tensor_tensor(out=seg[:], in0=seg[:], in1=bc(smask[:].unsqueeze(1), [128, 8, 128]), op=ALU.add), reads=[("seg", 0), ("seg", 1), "smask"], writes=[("seg", 0), ("seg", 1)])
                    op("act", lambda e: e.activation(out=seg[:], in_=seg[:], func=AF.Exp), reads=[("seg", 0), ("seg", 1)], writes=[("seg", 0), ("seg", 1)])
                    op("pe", lambda e, g_=g_: e.matmul(PB[5][:, 0:128], lhsT=xsb[:, 8 + g_, :], rhs=xsb[:, 10 + g_, :], start=True, stop=True), reads=[("xsb", 8 + g_), ("xsb", 10 + g_)], writes=["pb5", ("lg", 5, 0)])
                    op("dve", lambda e: e.tensor_tensor(out=MT[:], in0=seg[:], in1=bc(PB[5][:, 0:128].unsqueeze(1), [128, 8, 128]), op=ALU.mult), reads=[("seg", 0), ("seg", 1), "pb5"], writes=["MT"])
                    for hh in range(8):
                        hcol = slice((8 * g_ + hh) * 64, (8 * g_ + hh) * 64 + 64)
                        op("pe", lambda e, hh=hh, hcol=hcol, g_=g_: e.matmul(PB[g_][:, hh * 64:(hh + 1) * 64], lhsT=MT[:, hh, :], rhs=xdt[:, hcol], start=True, stop=True),
                           reads=["MT", "xdt"], writes=[("pb", g_, hh // 2)])
                    yo = 4 if g_ == 0 else 6
                    op("pe", lambda e, g_=g_, yo=yo, cols=cols: e.matmul(PB[yo][:, :], lhsT=xsb[:, 10 + g_, :], rhs=ssb[:, cols], start=True, stop=True),
                       reads=[("xsb", 10 + g_), "ssb", "pcs", "ptot", "pbv", "pbdt"] + [("pv", hd) for hd in range(8)], writes=[("yoff", g_)] + ([("pv", hd) for hd in range(8)] + ["pbv", "pbdt"] if g_ == 0 else ["pcs", "ptot"]))
                    op("pe", lambda e, g_=g_, cols=cols: e.matmul(PB[5][:, :], lhsT=btm[:, g_, :], rhs=xds[:, cols], start=True, stop=True), reads=["btm", "xds"], writes=["pb5", ("lg", 5, 0), ("lg", 5, 1)])
                    sv = ssf[:, cols].rearrange("p (h d) -> p h d", h=8)
                    op("pool", lambda e, sv=sv, hs_=hs_: e.tensor_tensor(out=sv, in0=sv, in1=bc(s16(7)[:, hs_].unsqueeze(2), [128, 8, 64]), op=ALU.mult), reads=["ssf", "cd"], writes=["ssf"])
                    op("dve", lambda e, cols=cols: e.tensor_tensor(out=ssf[:, cols], in0=ssf[:, cols], in1=PB[5][:, :], op=ALU.add), reads=["ssf", "pb5"], writes=["ssf"])
                    yv = y1[:, cols].rearrange("p (h d) -> p h d", h=8)
                    op("dve", lambda e, yv=yv, yo=yo, hs_=hs_: e.tensor_tensor(out=yv, in0=PB[yo][:, :].rearrange("p (h d) -> p h d", h=8), in1=bc(s16(5)[:, hs_].unsqueeze(2), [128, 8, 64]), op=ALU.mult), reads=[("yoff", g_), "ea"], writes=[("y1", g_), "ho"])
                    op("act", lambda e, g_=g_, cols=cols: e.copy(out=ssb[:, cols], in_=ssf[:, cols]), reads=["ssf", ("yoff", g_)], writes=["ssb"])
                    op("dve", lambda e, g_=g_, cols=cols: e.tensor_tensor(out=y1[:, cols], in0=y1[:, cols], in1=PB[g_][:, :], op=ALU.add), reads=[("y1", g_)] + [("pb", g_, b) for b in range(4)], writes=[("y1", g_)])
                    op("pool", lambda e, cols=cols, hs_=hs_: e.tensor_tensor(out=y2[:, cols].rearrange("p (h d) -> p h d", h=8), in0=xtm[:, cols].rearrange("p (h d) -> p h d", h=8), in1=bc(dsk[:, hs_].unsqueeze(2), [128, 8, 64]), op=ALU.mult), reads=["xtm", "dsk"], writes=[("y2", g_), ("r1", 0), ("r1", 1)])
                    op("pool", lambda e, g_=g_, cols=cols: e.tensor_tensor(out=y1[:, cols], in0=y1[:, cols], in1=y2[:, cols], op=ALU.add), reads=[("y1", g_), ("y2", g_)], writes=[("y1", g_)])
                    op("pool", lambda e, g_=g_, cols=cols: e.tensor_tensor(out=y1[:, cols], in0=y1[:, cols], in1=gz[:, cols], op=ALU.mult), reads=[("y1", g_), ("gz", g_)], writes=[("y1", g_)])
                    op("dve", lambda e, g_=g_: e.memset(rn[:, g_:g_ + 1], 0.0), writes=[("rn", g_)])
                    op("act", lambda e, g_=g_, cols=cols: e.activation(out=y2[:, cols], in_=y1[:, cols], func=AF.Square, accum_out=rn[:, g_:g_ + 1]), reads=[("y1", g_), ("rn", g_), ("y2", g_)], writes=[("y2", g_), ("rn", g_)])
                    op("dve", lambda e, g_=g_: e.tensor_scalar(out=rn[:, g_:g_ + 1], in0=rn[:, g_:g_ + 1], scalar1=0.25 / 512.0, scalar2=EPS, op0=ALU.mult, op1=ALU.add), reads=[("rn", g_)], writes=[("rn", g_)])
                    op("act", lambda e, g_=g_: e.activation(out=rn[:, g_:g_ + 1], in_=rn[:, g_:g_ + 1], func=AF.Ln), reads=[("rn", g_)], writes=[("rn", g_)])
                    op("act", lambda e, g_=g_: e.activation(out=rn[:, g_:g_ + 1], in_=rn[:, g_:g_ + 1], func=AF.Exp, scale=-0.5), reads=[("rn", g_)], writes=[("rn", g_)])
                    op("dve", lambda e, g_=g_, cols=cols: e.scalar_tensor_tensor(out=yb[:, cols], in0=y1[:, cols], scalar=rn[:, g_:g_ + 1], in1=ng[:, cols], op0=ALU.mult, op1=ALU.mult), reads=[("y1", g_), ("rn", g_), "ng"], writes=[("yb", g_), "hb"])
                for kc in range(8):
                    op("pe", lambda e, kc=kc: e.transpose(out=PT[:, kc * 128:(kc + 1) * 128], in_=yb[:, kc * 128:(kc + 1) * 128], identity=ident[:]), reads=[("yb", 0), ("yb", 1), "ident"], writes=[("pt", kc)])
                op("act", lambda e: e.copy(out=ymT[:, 4:12, :], in_=PTv), reads=[("pt", k) for k in range(8)], writes=["ymT_b"])
                YK = ["ymT_b", "ymT_c"] + [("ymT_a", b) for b in range(4)]
                for hf in range(2):
                    for kc in range(16):
                        op("pe", lambda e, hf=hf, kc=kc: e.matmul(PB[hf][:, :], lhsT=ymT[:, kc, :], rhs=Wo[:, kc, 512 * hf:512 * (hf + 1)], start=(kc == 0), stop=(kc == 15)),
                           reads=YK + WOK, writes=[("pb", hf, b) for b in range(4)])
                    op("dve", lambda e, hf=hf: e.scalar_tensor_tensor(out=r1[:, 512 * hf:512 * (hf + 1)], in0=h[:, 512 * hf:512 * (hf + 1)], scalar=ALPHA, in1=PB[hf][:, :], op0=ALU.mult, op1=ALU.add),
                       reads=["h"] + [("pb", hf, b) for b in range(4)], writes=[("r1", hf), ("y2", 0), ("y2", 1)])
                ln_rows(lnst, r1, ho, l1g, l1b, "l1g", "l1b", "m", [("r1", 0), ("r1", 1)], ["ho", ("y1", 0), ("y1", 1)])
                op("sp", lambda e, rows=rows: e.dma_start(out=hs1[rows, :], in_=ho[:]), reads=["ho"], writes=[("hs1", ci)], dma="ho")


def peer_phase(nc, P, es_outer, E, l, last):
    op = P.op
    g = E
    NSEQ, S, NT, TT, NEB = g["NSEQ"], g["S"], g["NT"], g["TT"], g["NEB"]
    PB, PT = g["PB"], g["PT"]
    ident, iocb, io16 = g["ident"], g["iocb"], g["io16"]
    hs0, hs1, out, UT, VB = g["hs0"], g["hs1"], g["out"], g["UT"], g["VB"]
    lnst, ln_rows = g["lnst"], g["ln_rows"]
    dst = out if last else hs0
    G = 4
    NSUB = TT // 128
    PTv = PT[:].rearrange("p (k e) -> p k e", k=8)
    with contextlib.ExitStack() as st:
        def sb(name, shape, dt=F32):
            return st.enter_context(nc.sbuf_tensor(f"{name}_p{l}", list(shape), dt))
        Wq = sb("Wq", [128, 8, D], BF16)
        stq_ = sb("stq0", [128, D]); stg = [stq_, stq_]
        kst = sb("kst", [128, 256]); KT = sb("KT", [128, 256], BF16)
        l2g, l2b = sb("l2g", [128, D]), sb("l2b", [128, D])
        WT = sb("WT", [128, 128, TT], BF16)
        h1T = [sb(f"h1T{i}", [128, 8, TT], BF16) for i in range(2)]
        hq = sb("hq", [128, D]); hqb = sb("hqb", [128, D], BF16)
        qTs = sb("qTs", [128, 8, 128], BF16)
        sc = sb("sc", [128, 16, 128]); scr = sb("scr", [128, 256])
        V16 = sb("V16", [128, 16, 16]); I16 = sb("I16", [128, 16, 16], U32); If = sb("If", [128, 16, 16])
        cs2 = sb("cs2", [128, 8, 256]); eq = sb("eq", [128, 8, 256])
        BV = sb("BV", [128, 8, 16]); BP = sb("BP", [128, 8, 16], U32); BPa = sb("BPa", [128, 8, 16], U32); BPb = sb("BPb", [128, 8, 16], U32)
        apos = sb("apos", [128, 8, 16]); bpos = sb("bpos", [128, 8, 16]); gs = sb("gs", [128, 8, 16]); zz = sb("zz", [128, 8, 2])
        i0s = sb("i0s", [128, 8, 16]); i1s = sb("i1s", [128, 8, 16])
        selb = sb("selb", [128, 3, 128], BF16); selT = [sb(f"selT{i}", [128, 3, TT], BF16) for i in range(2)]
        ohA = [sb(f"ohA{i}", [128, 16, 64], BF16) for i in range(3)]
        ohB = [sb(f"ohB{i}", [128, 16, 128], BF16) for i in range(3)]
        Ug = [sb(f"Ug{i}", [128, G, 8, 128], BF16) for i in range(2)]
        Vg = [sb(f"Vg{i}", [128, G, D], BF16) for i in range(2)]
        ge = [sb(f"ge{i}", [128, TT], BF16) for i in range(2)]
        gW = [sb(f"gW{i}", [128, G, TT], BF16) for i in range(2)]
        r1 = sc[:, 0:8, :].rearrange("p a b -> p (a b)"); ho = sc[:, 8:16, :].rearrange("p a b -> p (a b)")
        wv = g["peer_wq"][l].rearrange("(kc p) n -> p kc n", p=128)
        for kc in range(8):
            i = 0
            op("sp", lambda e, kc=kc, i=i: e.dma_start(out=stg[i][:], in_=wv[:, kc, :]), writes=[f"stq{i}"], dma=f"stq{i}")
            op("act", lambda e, kc=kc, i=i: e.copy(out=Wq[:, kc, :], in_=stg[i][:]), reads=[f"stq{i}"], writes=[("Wq", kc)])
        WQK = [("Wq", kc) for kc in range(8)]
        op("dve", lambda e: e.memset(kst[:], 0.0), writes=["kst"])
        for i in range(2):
            op("sp", lambda e, i=i: e.dma_start(out=kst[i * 64:(i + 1) * 64, i * 128:(i + 1) * 128], in_=g["peer_keys"][l, i].rearrange("k d -> d k"), allow_slow_non_contiguous=True), reads=[], writes=["kst"], dma="c_kst")
        op("dve", lambda e: e.tensor_copy(out=KT[:], in_=kst[:]), reads=["kst"], writes=["KT"])
        op("sp", lambda e: e.dma_start(out=l2g[:], in_=g["ln2_g"][l].partition_broadcast(128)), writes=["l2g"], dma="c_l2g")
        op("sp", lambda e: e.dma_start(out=l2b[:], in_=g["ln2_b"][l].partition_broadcast(128)), writes=["l2b"], dma="c_l2b")
        V16v = V16[:].rearrange("p (h i) k -> p h i k", i=2)
        Ifv = If[:].rearrange("p (h i) k -> p h i k", i=2)
        cs4 = cs2[:].rearrange("p h (a b) -> p h a b", a=16)
        eq4 = eq[:].rearrange("p h (a b) -> p h a b", a=16)
        NTILE = NT // TT
        ng_ = NEB // G
        HK = lambda sl: [("h1T", sl, cc) for cc in range(NSUB)]
        WK = [("WT", cc) for cc in range(NSUB)]

        def sel_steps(tile):
            sl = tile % 2
            t0 = tile * TT
            steps = []
            for cc in range(NSUB):
                ci = (t0 + cc * 128) // 128
                rows = slice(ci * 128, (ci + 1) * 128)
                tcols = slice(cc * 128, (cc + 1) * 128)

                def s_load0(rows=rows, ci=ci, tcols=tcols, cc=cc):
                    op("sp", lambda e: e.dma_start(out=hq[:], in_=hs1[rows, :]), reads=[("hs1", ci)], writes=["hq"], dma="hq")
                    op("act", lambda e: e.copy(out=hqb[:], in_=hq[:]), reads=["hq"], writes=["hqb"])
                steps.append((False, False, s_load0))

                def s_load(rows=rows, ci=ci, tcols=tcols, cc=cc):
                    for kc in range(8):
                        op("pe", lambda e, kc=kc: e.transpose(out=PT[:, kc * 128:(kc + 1) * 128], in_=hqb[:, kc * 128:(kc + 1) * 128], identity=ident[:]), reads=["hqb", "ident"], writes=[("pt", kc)])
                    op("act", lambda e: e.copy(out=h1T[sl][:, :, tcols], in_=PTv), reads=[("pt", k) for k in range(8)], writes=[("h1T", sl, cc)])
                steps.append((True, True, s_load))

                def s_q(half, tcols=tcols, cc=cc):
                    def f():
                        for hd in range(4 * half, 4 * half + 4):
                            for kc in range(8):
                                op("pe", lambda e, hd=hd, kc=kc: e.matmul(PB[6][:, (hd % 4) * 128:(hd % 4 + 1) * 128], lhsT=Wq[:, kc, hd * 128:(hd + 1) * 128], rhs=h1T[sl][:, kc, tcols], start=(kc == 0), stop=(kc == 7)),
                                   reads=[("h1T", sl, cc)] + WQK, writes=[("pp", 6)])
                        op("act", lambda e: e.copy(out=qTs[:, 4 * half:4 * half + 4, :], in_=PB[6][:, :].rearrange("p (a b) -> p a b", a=4)), reads=[("pp", 6)], writes=[("qTs", half)])
                    return f
                steps.append((True, True, s_q(0))); steps.append((False, True, s_q(1)))

                def s_sc(j):
                    def f():
                        for hd in (2 * j, 2 * j + 1):
                            op("pe", lambda e, hd=hd: e.matmul(PB[6][:, (hd % 2) * 256:(hd % 2 + 1) * 256], lhsT=qTs[:, hd, :], rhs=KT[:], start=True, stop=True), reads=[("qTs", hd // 4), "KT"], writes=[("pp", 6)])
                        op("act", lambda e: e.copy(out=sc[:, 4 * j:4 * j + 4, :], in_=PB[6][:, :].rearrange("p (a b) -> p a b", a=4)), reads=[("pp", 6)], writes=[("sc", j), ("r1p", 0), ("r1p", 1), "hop"])
                    return f
                for j in range(4):
                    steps.append((j == 0, True, s_sc(j)))

                def s_top(hi):
                    def f():
                        sk = ("sc", hi // 4)
                        op("dve", lambda e: e.max(out=V16[:, hi, 0:8], in_=sc[:, hi, :]), reads=[sk], writes=["v8a"])
                        op("dve", lambda e: e.max_index(out=I16[:, hi, 0:8], in_max=V16[:, hi, 0:8], in_values=sc[:, hi, :]), reads=[sk, "v8a"], writes=["I16"])
                        op("dve", lambda e: e.match_replace(out=scr[:, 0:128], in_to_replace=V16[:, hi, 0:8], in_values=sc[:, hi, :], imm_value=-1e30), reads=[sk, "v8a"], writes=["scr"])
                        op("dve", lambda e: e.max(out=V16[:, hi, 8:16], in_=scr[:, 0:128]), reads=["scr"], writes=["v8b"])
                        op("dve", lambda e: e.max_index(out=I16[:, hi, 8:16], in_max=V16[:, hi, 8:16], in_values=scr[:, 0:128]), reads=["scr", "v8b"], writes=["I16"])
                    return f
                for hi in range(16):
                    steps.append((False, False, s_top(hi)))

                def s_cand():
                    op("dve", lambda e: e.tensor_copy(out=If[:], in_=I16[:]), reads=["I16"], writes=["If"])
                    op("dve", lambda e: e.tensor_tensor(out=cs4, in0=bc(V16v[:, :, 0, :].unsqueeze(3), [128, 8, 16, 16]), in1=bc(V16v[:, :, 1, :].unsqueeze(2), [128, 8, 16, 16]), op=ALU.add), reads=["v8a", "v8b"], writes=["cs2"])
                steps.append((False, False, s_cand))

                def s_top2(hd):
                    def f():
                        op("dve", lambda e: e.max(out=BV[:, hd, 0:8], in_=cs2[:, hd, :]), reads=["cs2"], writes=["b8a"])
                        op("dve", lambda e: e.max_index(out=BP[:, hd, 0:8], in_max=BV[:, hd, 0:8], in_values=cs2[:, hd, :]), reads=["cs2", "b8a"], writes=["BP"])
                        op("dve", lambda e: e.match_replace(out=scr[:, :], in_to_replace=BV[:, hd, 0:8], in_values=cs2[:, hd, :], imm_value=-1e30), reads=["cs2", "b8a"], writes=["scr"])
                        op("dve", lambda e: e.max(out=BV[:, hd, 8:16], in_=scr[:, :]), reads=["scr"], writes=["b8b"])
                        op("dve", lambda e: e.max_index(out=BP[:, hd, 8:16], in_max=BV[:, hd, 8:16], in_values=scr[:, :]), reads=["scr", "b8b"], writes=["BP"])
                    return f
                for hd in range(8):
                    steps.append((False, False, s_top2(hd)))

                def s_gate():
                    op("dve", lambda e: e.tensor_tensor(out=gs[:], in0=BV[:], in1=bc(BV[:, :, 0:1], [128, 8, 16]), op=ALU.subtract), reads=["b8a", "b8b"], writes=["gs"])
                    op("act", lambda e: e.activation(out=gs[:], in_=gs[:], func=AF.Exp), reads=["gs"], writes=["gs"])
                    op("dve", lambda e: e.reduce_sum(out=zz[:, :, 0], in_=gs[:], axis=AX.X), reads=["gs"], writes=["zz0"])
                    op("dve", lambda e: e.reciprocal(out=zz[:, :, 1], in_=zz[:, :, 0]), reads=["zz0"], writes=["zz1"])
                    op("dve", lambda e: e.tensor_tensor(out=gs[:], in0=gs[:], in1=bc(zz[:, :, 1:2], [128, 8, 16]), op=ALU.mult), reads=["gs", "zz1"], writes=["gs"])
                    op("dve", lambda e: e.tensor_single_scalar(out=BPa[:], in_=BP[:], scalar=4, op=ALU.logical_shift_right), reads=["BP"], writes=["BPa"])
                    op("dve", lambda e: e.tensor_single_scalar(out=BPb[:], in_=BP[:], scalar=15, op=ALU.bitwise_and), reads=["BP"], writes=["BPb"])
                    op("dve", lambda e: e.tensor_copy(out=apos[:], in_=BPa[:]), reads=["BPa"], writes=["apos"])
                    op("dve", lambda e: e.tensor_copy(out=bpos[:], in_=BPb[:]), reads=["BPb"], writes=["bpos"])
                steps.append((False, False, s_gate))

                def s_idx(which):
                    def f():
                        pos, pk, half, dst_, dk = ((apos, "apos", 0, i0s, "i0s"), (bpos, "bpos", 1, i1s, "i1s"))[which]
                        op("dve", lambda e: e.tensor_tensor(out=eq4, in0=bc(io16[:].unsqueeze(1).unsqueeze(1), [128, 8, 16, 16]), in1=bc(pos[:].unsqueeze(3), [128, 8, 16, 16]), op=ALU.is_equal), reads=["io16", pk], writes=["eq"])
                        op("dve", lambda e: e.tensor_tensor(out=eq4, in0=eq4, in1=bc(Ifv[:, :, half, :].unsqueeze(2), [128, 8, 16, 16]), op=ALU.mult), reads=["eq", "If"], writes=["eq"])
                        op("dve", lambda e: e.reduce_sum(out=dst_[:], in_=eq4, axis=AX.X), reads=["eq"], writes=[dk])
                    return f
                steps.append((False, False, s_idx(0))); steps.append((False, False, s_idx(1)))

                def s_selT0(tcols=tcols, cc=cc):
                    op("act", lambda e: e.copy(out=selb[:, 0, :], in_=i0s[:].rearrange("p h k -> p (h k)")), reads=["i0s"], writes=["selb0"])
                    op("act", lambda e: e.copy(out=selb[:, 1, :], in_=i1s[:].rearrange("p h k -> p (h k)")), reads=["i1s"], writes=["selb1"])
                    op("act", lambda e: e.copy(out=selb[:, 2, :], in_=gs[:].rearrange("p h k -> p (h k)")), reads=["gs"], writes=["selb2"])
                steps.append((False, False, s_selT0))

                def s_selT(tcols=tcols, cc=cc):
                    for k in range(3):
                        op("pe", lambda e, k=k: e.transpose(out=PT[:, k * 128:(k + 1) * 128], in_=selb[:, k, :], identity=ident[:]), reads=[f"selb{k}", "ident"], writes=[("pt", k)])
                    op("act", lambda e: e.copy(out=selT[sl][:, :, tcols], in_=PTv[:, 0:3, :]), reads=[("pt", k) for k in range(3)], writes=[("selT", sl, cc)])
                steps.append((True, True, s_selT))
            return steps

        nWc = [0]
        PTf = PT[:].bitcast(F32)

        def build_steps(tile, half):
            sl = tile % 2
            i0c = slice(half * 64, half * 64 + 64)
            pres, pes = [], []
            for cc in range(NSUB):
                for tb in range(8):
                    o_ = nWc[0] % 3
                    nWc[0] += 1
                    toks = slice(cc * 128 + tb * 16, cc * 128 + tb * 16 + 16)

                    def pre(cc=cc, o_=o_, toks=toks):
                        iob = bc(iocb[:].unsqueeze(1), [128, 16, 128])
                        ioh = bc(iocb[:, i0c].unsqueeze(1), [128, 16, 64])
                        op("dve", lambda e: e.tensor_tensor(out=ohA[o_][:], in0=ioh, in1=bc(selT[sl][:, 0, toks].unsqueeze(2), [128, 16, 64]), op=ALU.is_equal), reads=["iocb", ("selT", sl, cc)], writes=[f"ohA{o_}"])
                        op("dve", lambda e: e.tensor_tensor(out=ohA[o_][:], in0=ohA[o_][:], in1=bc(selT[sl][:, 2, toks].unsqueeze(2), [128, 16, 64]), op=ALU.mult), reads=[f"ohA{o_}", ("selT", sl, cc)], writes=[f"ohA{o_}"])
                        op("dve", lambda e: e.tensor_tensor(out=ohB[o_][:], in0=iob, in1=bc(selT[sl][:, 1, toks].unsqueeze(2), [128, 16, 128]), op=ALU.is_equal), reads=["iocb", ("selT", sl, cc)], writes=[f"ohB{o_}"])

                    def pe(cc=cc, tb=tb, o_=o_):
                        for q2 in range(2):
                            if q2 == 0:
                                dst_ps, pk = PTf, [("pt", k) for k in range(4)]
                            else:
                                dst_ps, pk = PB[6], [("pp", 6)]
                            for tt in range(8):
                                tl = q2 * 8 + tt
                                op("pe", lambda e, tl=tl, tt=tt, dst_ps=dst_ps: e.matmul(dst_ps[:, tt * 64:(tt + 1) * 64], lhsT=ohB[o_][:, tl, :], rhs=ohA[o_][:, tl, :], start=True, stop=True), reads=[f"ohA{o_}", f"ohB{o_}"], writes=pk)
                            tk0 = cc * 128 + tb * 16 + q2 * 8
                            op("act", lambda e, tk0=tk0, dst_ps=dst_ps: e.copy(out=WT[:, i0c, tk0:tk0 + 8].rearrange("p i t -> p t i"), in_=dst_ps[:, :].rearrange("p (t i) -> p t i", t=8)), reads=pk, writes=[("WT", half, cc)])
                    pres.append(pre); pes.append(pe)
            steps = []
            for k in range(0, len(pres), 2):
                steps.append((False, False, pres[k])); steps.append((False, False, pres[k + 1]))
                steps.append((True, True, pes[k])); steps.append((False, True, pes[k + 1]))
            return steps

        def group_steps(steps, max_nonpe):
            groups = [[]]
            n_nonpe = 0
            for newgrp, pe_first, fn in steps:
                if pe_first:
                    if newgrp or n_nonpe > 0:
                        if groups[-1]:
                            groups.append([])
                        n_nonpe = 0
                else:
                    if n_nonpe >= max_nonpe:
                        groups.append([])
                        n_nonpe = 0
                    n_nonpe += 1
                groups[-1].append((pe_first, fn))
            return groups

        nac = [0]

        def act_group(tile, gi):
            sl = tile % 2
            gs_ = gi % 2
            ebs = slice(gi * G, (gi + 1) * G)
            wk = [("WT", (gi * G) // 64, cc) for cc in range(NSUB)]
            op("sp", lambda e: e.dma_start(out=Ug[gs_][:], in_=UT[l, ebs].rearrange("g p k e -> p g k e")), reads=[("UT", l, eb) for eb in range(gi * G, (gi + 1) * G)], writes=[f"Ug{gs_}"], dma=f"Ug{gs_}")
            op("sp", lambda e: e.dma_start(out=Vg[gs_][:], in_=VB[l, ebs].rearrange("g p d -> p g d")), reads=[("VB", l, eb) for eb in range(gi * G, (gi + 1) * G)], writes=[f"Vg{gs_}"], dma=f"Vg{gs_}")
            for gl in range(G):
                eb = gi * G + gl
                ab = nac[0] % 2
                nac[0] += 1
                for kc in range(8):
                    op("pe", lambda e, gl=gl, kc=kc, ab=ab: e.matmul(PB[ab][:, 0:TT], lhsT=Ug[gs_][:, gl, kc, :], rhs=h1T[sl][:, kc, :], start=(kc == 0), stop=(kc == 7)), reads=[f"Ug{gs_}"] + HK(sl), writes=[("pp", ab)])
                op("act", lambda e, ab=ab: e.activation(out=ge[ab][:], in_=PB[ab][:, 0:TT], func=AF.Gelu), reads=[("pp", ab)], writes=[f"ge{ab}"])
                op("pool", lambda e, ab=ab, gl=gl, eb=eb: e.tensor_tensor(out=gW[gs_][:, gl, :], in0=ge[ab][:], in1=WT[:, eb, :], op=ALU.mult), reads=[f"ge{ab}"] + wk, writes=[(f"gW{gs_}", gl)])

        def out_group(tile, gi):
            gs_ = gi % 2
            for sub in range(NSUB):
                for hf in range(2):
                    bk = 2 + sub * 2 + hf
                    for gl in range(G):
                        op("pe", lambda e, gl=gl, sub=sub, hf=hf, bk=bk: e.matmul(PB[bk][:, :], lhsT=gW[gs_][:, gl, sub * 128:(sub + 1) * 128], rhs=Vg[gs_][:, gl, hf * 512:(hf + 1) * 512], start=(gi == 0 and gl == 0), stop=(gi == ng_ - 1 and gl == G - 1)),
                           reads=[(f"gW{gs_}", gl), f"Vg{gs_}"], writes=[("pp", bk)])

        def tail(tile):
            t0 = tile * TT
            for cc in range(NSUB):
                ci = (t0 + cc * 128) // 128
                rows = slice(ci * 128, (ci + 1) * 128)
                op("sp", lambda e, rows=rows: e.dma_start(out=hq[:], in_=hs1[rows, :]), reads=[("hs1", ci)], writes=["hq"], dma="hq")
                for hf in range(2):
                    op("dve", lambda e, cc=cc, hf=hf: e.scalar_tensor_tensor(out=r1[:, hf * 512:(hf + 1) * 512], in0=hq[:, hf * 512:(hf + 1) * 512], scalar=ALPHA, in1=PB[2 + cc * 2 + hf][:, :], op0=ALU.mult, op1=ALU.add),
                       reads=["hq", ("pp", 2 + cc * 2 + hf)], writes=[("r1p", hf)] + [("sc", j) for j in range(4)])
                ln_rows(lnst, r1, ho, l2g, l2b, "l2g", "l2b", "p", [("r1p", 0), ("r1p", 1)], ["hop"] + [("sc", j) for j in range(4)])
                op("sp", lambda e, rows=rows: e.dma_start(out=dst[rows, :], in_=ho[:]), reads=["hop"], writes=[("hs0", ci)], dma="hop")

        assert NSUB == 2 and ng_ % 2 == 0
        hg = ng_ // 2
        for _, _, f in sel_steps(0):
            f()
        for _, _, f in build_steps(0, 0):
            f()

        def fit(groups, n):
            while len(groups) > n:
                last = groups.pop()
                groups[-1].extend(last)
            return groups + [[] for _ in range(n - len(groups))]

        for tile in range(NTILE):
            has_next = tile + 1 < NTILE
            n1 = hg - 1
            chA = fit(group_steps(build_steps(tile, 1), 4), n1)
            chS = fit(group_steps(sel_steps(tile + 1), 10), n1) if has_next else [[] for _ in range(n1)]
            chB = fit(group_steps(build_steps(tile + 1, 0), 4), n1) if has_next else [[] for _ in range(n1)]
            act_group(tile, 0)
            for gi in range(ng_):
                if gi < n1:
                    todo = chA[gi] + chS[gi]
                elif hg <= gi < hg + n1:
                    todo = chB[gi - hg]
                else:
                    todo = []
                pes_ = [f for pe_first, f in todo if pe_first]
                for f in pes_[:1]:
                    f()
                if gi + 1 < ng_:
                    act_group(tile, gi + 1)
                for f in pes_[1:]:
                    f()
                out_group(tile, gi)
                for pe_first, f in todo:
                    if not pe_first:
                        f()
            tail(tile)


_NC_CACHE = {}


def kernel(**inputs):
    NCORES = 8
    x = np.asarray(inputs["x"], dtype=np.float32)
    B, S, _ = x.shape
    NSEQ = B // NCORES
    depth = inputs["w_in"].shape[0]
    key = (NSEQ, S, depth)
    if key not in _NC_CACHE:
        _NC_CACHE[key] = build_nc(NSEQ, S, depth, TT=256)
    nc = _NC_CACHE[key]
    in_maps = []
    for c in range(NCORES):
        m = {k: np.ascontiguousarray(np.asarray(v, dtype=np.float32)) for k, v in inputs.items() if k != "x"}
        m["x"] = np.ascontiguousarray(x[c * NSEQ:(c + 1) * NSEQ].reshape(NSEQ * S, D))
        in_maps.append(m)
    res = run_bass_kernel_spmd(nc, in_maps, core_ids=list(range(NCORES)))
    outs = [np.asarray(r["out"]).reshape(NSEQ, S, D) for r in res.results]
    return np.concatenate(outs, axis=0).astype(np.float32)
```

```python
import contextlib
import numpy as np
import concourse.bass as bass
import concourse.mybir as mybir
from concourse.bass_utils import run_bass_kernel_spmd

F32 = mybir.dt.float32
BF16 = mybir.dt.bfloat16
U32 = mybir.dt.uint32
AF = mybir.ActivationFunctionType
ALU = mybir.AluOpType
AX = mybir.AxisListType

D = 1024
INC = 4368
C_RGX, C_RGG, C_Z, C_XBC, C_DT, C_Q, C_K, C_V = 0, 512, 1024, 2048, 3584, 3600, 4112, 4240
ALPHA = 4.0 ** 0.25
EPS = 1e-5
EPOCH = 16000
NEG = -1.0e5


class Prog:
    ENG = ("pe", "dve", "act", "pool", "sp")

    def __init__(self, nc):
        self.nc = nc
        self.ops = []
        self.last_w = {}
        self.readers = {}
        self.dma_cum = {}
        self.bar = set()
        self.bar_seen = set()
        self.bank_w = {}
        self.bank_r = {}

    @staticmethod
    def bank_of(k):
        if isinstance(k, tuple):
            t = k[0]
            if t == "pb" or t == "pp":
                return k[1]
            if t == "pbz":
                return 2 + k[1]
            if t == "lg":
                return k[1]
            if t == "pv":
                return 4
            if t == "pt":
                return 7
            if t == "yoff":
                return 4 if k[1] == 0 else 6
            return None
        return {"pbv": 4, "pbdt": 4, "pcs": 6, "ptot": 6, "pb5": 5}.get(k)

    chain = None

    def merge(self, a, b):
        na, nb = len(a), len(b)
        ia = ib = 0
        while ia < na or ib < nb:
            if ib >= nb or (ia < na and ia * nb <= ib * na):
                self.op(*a[ia]); ia += 1
            else:
                self.op(*b[ib]); ib += 1

    def op(self, eng, fn, reads=(), writes=(), dma=None, ndma=1):
        if self.chain is not None:
            self.chain.append((eng, fn, tuple(reads), tuple(writes), dma, ndma))
            return None
        i = len(self.ops)
        deps = set()
        for k in reads:
            b = self.bank_of(k)
            if b is not None:
                if b in self.bank_w:
                    deps.add(self.bank_w[b])
                self.bank_r.setdefault(b, []).append(i)
        if eng == "pe":
            for k in writes:
                b = self.bank_of(k)
                if b is not None:
                    deps |= set(self.bank_r.get(b, ()))
                    self.bank_r[b] = []
                    self.bank_w[b] = i
        for k in reads:
            if k in self.last_w:
                deps.add(self.last_w[k])
        for k in writes:
            if k in self.last_w:
                deps.add(self.last_w[k])
            for r in self.readers.get(k, ()):
                deps.add(r)
        if self.bar and eng not in self.bar_seen:
            deps |= self.bar
            self.bar_seen.add(eng)
        deps.discard(i)
        if eng == "pe":
            deps = {d for d in deps if self.ops[d]["eng"] != "pe"}
        for k in writes:
            self.last_w[k] = i
            self.readers[k] = []
        for k in reads:
            self.readers.setdefault(k, []).append(i)
        o = dict(eng=eng, fn=fn, deps=deps, dma=dma, ndma=ndma, sig=False, cnt=None)
        if dma is not None:
            c = self.dma_cum.get(dma, 0) + 16 * ndma
            self.dma_cum[dma] = c
            o["cnt"] = c
            assert c < 60000, dma
        self.ops.append(o)
        return i

    def barrier(self):
        last = {}
        for i, o in enumerate(self.ops):
            if o["dma"] is not None:
                last[("d", o["dma"])] = i
            else:
                last[("c", o["eng"])] = i
        self.bar = set(last.values())
        self.bar_seen = set()

    def emit(self):
        nc = self.nc
        ops = self.ops
        for o in ops:
            for d in o["deps"]:
                ops[d]["sig"] = True
        cnt = {e: 0 for e in self.ENG}
        for o in ops:
            if o["dma"] is None and o["sig"]:
                cnt[o["eng"]] += 1
                o["cnt"] = cnt[o["eng"]]
        nep = {e: cnt[e] // EPOCH + 1 for e in self.ENG}
        with contextlib.ExitStack() as es:
            csem = {e: [es.enter_context(nc.semaphore(f"c_{e}_{j}")) for j in range(nep[e])] for e in self.ENG}
            dsem = {n: es.enter_context(nc.semaphore(f"d_{n}")) for n in self.dma_cum}
            block = es.enter_context(nc.Block())

            def semval(o):
                if o["dma"] is not None:
                    return dsem[o["dma"]], o["cnt"]
                c = o["cnt"]
                ep = (c - 1) // EPOCH
                return csem[o["eng"]][ep], c - ep * EPOCH

            def run(engname, eng):
                waited = {}
                for o in ops:
                    if o["eng"] != engname:
                        continue
                    need = {}
                    for d in o["deps"]:
                        s, v = semval(ops[d])
                        key = id(s)
                        if need.get(key, (None, 0))[1] < v:
                            need[key] = (s, v)
                    for key, (s, v) in need.items():
                        if waited.get(key, 0) >= v:
                            continue
                        waited[key] = v
                        eng.wait_ge(s, v)
                    ins = o["fn"](eng)
                    if o["dma"] is not None:
                        lst = ins if isinstance(ins, (list, tuple)) else [ins]
                        assert len(lst) == o["ndma"], (len(lst), o["ndma"])
                        s, _ = semval(o)
                        for x in lst:
                            x.then_inc(s, 16)
                    elif o["sig"]:
                        s, _ = semval(o)
                        ins.then_inc(s, 1)
                if engname == "sp":
                    for n, s in dsem.items():
                        eng.wait_ge(s, self.dma_cum[n])

            @block.sync
            def _(e):
                run("sp", e)

            @block.tensor
            def _(e):
                run("pe", e)

            @block.vector
            def _(e):
                run("dve", e)

            @block.scalar
            def _(e):
                run("act", e)

            @block.gpsimd
            def _(e):
                run("pool", e)


def bc(ap, shape):
    return ap.to_broadcast(list(shape))


def build_nc(NSEQ, S, DEPTH, TT=512, NEXP_BLK=128):
    nc = bass.Bass("TRN2", target_bir_lowering=False)
    NT = NSEQ * S
    NCH = S // 128
    P = Prog(nc)
    op = P.op

    def din(name, shape):
        return nc.dram_tensor(name, list(shape), F32, kind="ExternalInput").ap()

    x = din("x", [NT, D])
    emb_g, emb_b = din("emb_ln_g", [D]), din("emb_ln_b", [D])
    w_in = din("w_in", [DEPTH, D, INC])
    rg_conv_w, rg_conv_b = din("rg_conv_w", [DEPTH, 4, 512]), din("rg_conv_b", [DEPTH, 512])
    rg_wa, rg_ba = din("rg_wa", [DEPTH, 8, 64, 64]), din("rg_ba", [DEPTH, 512])
    rg_wx, rg_bx = din("rg_wx", [DEPTH, 8, 64, 64]), din("rg_bx", [DEPTH, 512])
    rg_lambda = din("rg_lambda", [DEPTH, 512])
    ssm_conv_w, ssm_conv_b = din("ssm_conv_w", [DEPTH, 4, 1536]), din("ssm_conv_b", [DEPTH, 1536])
    ssm_dt_bias, ssm_a_log, ssm_d = din("ssm_dt_bias", [DEPTH, 16]), din("ssm_a_log", [DEPTH, 16]), din("ssm_d", [DEPTH, 16])
    ssm_norm_g = din("ssm_norm_g", [DEPTH, D])
    attn_sinks = din("attn_sinks", [DEPTH, 8])
    w_out = din("w_out", [DEPTH, 2048, D])
    ln1_g, ln1_b = din("ln1_g", [DEPTH, D]), din("ln1_b", [DEPTH, D])
    peer_wq = din("peer_wq", [DEPTH, D, D])
    peer_keys = din("peer_keys", [DEPTH, 2, 128, 64])
    peer_u, peer_v = din("peer_u", [DEPTH, 16384, D]), din("peer_v", [DEPTH, 16384, D])
    ln2_g, ln2_b = din("ln2_g", [DEPTH, D]), din("ln2_b", [DEPTH, D])
    out = nc.dram_tensor("out", [NT, D], F32, kind="ExternalOutput").ap()
    hs0 = nc.dram_tensor("hs0", [NT, D], F32, kind="Internal").ap()
    hs1 = nc.dram_tensor("hs1", [NT, D], F32, kind="Internal").ap()
    NEB = NEXP_BLK
    UT = nc.dram_tensor("UTs", [DEPTH, NEB, 128, 8, 128], BF16, kind="Internal").ap()
    VB = nc.dram_tensor("VBs", [DEPTH, NEB, 128, D], BF16, kind="Internal").ap()

    es = contextlib.ExitStack()
    with es:
        def sb(name, shape, dt=F32, st=es):
            return st.enter_context(nc.sbuf_tensor(name, list(shape), dt))

        def ps(name, shape, dt=F32):
            return es.enter_context(nc.psum_tensor(name, list(shape), dt))

        PB = [ps(f"pb{i}", [128, 512]) for i in range(7)]
        PT = ps("pt", [128, 1024], BF16)
        ident = sb("ident", [128, 128], BF16)
        iof = sb("iof", [128, 128])
        iocol = sb("iocol", [128, 128])
        iocb = sb("iocb", [128, 128], BF16)
        triu = sb("triu", [128, 128])
        ones = sb("ones", [128, 128])
        smask = sb("smask", [128, 128])
        amask = sb("amask", [128, 2, 256])
        io16 = sb("io16", [128, 16])

        op("pool", lambda e: e.iota(iof[:], pattern=[[1, 128]], base=0, channel_multiplier=-1, allow_small_or_imprecise_dtypes=True), writes=["iof"])
        op("pool", lambda e: e.iota(iocol[:], pattern=[[1, 128]], base=0, channel_multiplier=0, allow_small_or_imprecise_dtypes=True), writes=["iocol"])
        op("pool", lambda e: e.iota(io16[:], pattern=[[1, 16]], base=0, channel_multiplier=0, allow_small_or_imprecise_dtypes=True), writes=["io16"])
        op("dve", lambda e: e.tensor_copy(out=iocb[:], in_=iocol[:]), reads=["iocol"], writes=["iocb"])
        op("dve", lambda e: e.tensor_single_scalar(out=ident[:], in_=iof[:], scalar=0.0, op=ALU.is_equal), reads=["iof"], writes=["ident"])
        op("dve", lambda e: e.tensor_single_scalar(out=triu[:], in_=iof[:], scalar=0.0, op=ALU.is_ge), reads=["iof"], writes=["triu"])
        op("dve", lambda e: e.memset(ones[:], 1.0), writes=["ones"])
        op("dve", lambda e: e.tensor_scalar(out=smask[:], in0=triu[:], scalar1=-1.0, scalar2=-NEG, op0=ALU.add, op1=ALU.mult), reads=["triu"], writes=["smask"])
        op("dve", lambda e: e.memset(amask[:, 0, 0:128], NEG), writes=["amask0a"])
        op("dve", lambda e: e.tensor_scalar(out=amask[:, 1, 0:128], in0=iof[:], scalar1=0.0, scalar2=NEG, op0=ALU.is_le, op1=ALU.mult), reads=["iof"], writes=["amask1a"])
        for v_ in range(2):
            op("dve", lambda e, v_=v_: e.tensor_scalar(out=amask[:, v_, 128:256], in0=iof[:], scalar1=0.0, scalar2=NEG, op0=ALU.is_gt, op1=ALU.mult), reads=["iof"], writes=[f"amask{v_}b"])
        AMK = ["amask0a", "amask1a", "amask0b", "amask1b"]

        def ln_rows(st, src, dst, gt, bt, gk, bk, tag, src_keys, dst_keys):
            stats, mv, rs = st["stats"], st["mv"], st["rs"]
            for j in range(2):
                op("dve", lambda e, j=j: e.bn_stats(out=stats[:, j, :], in_=src[:, j * 512:(j + 1) * 512]), reads=src_keys, writes=[f"stats{j}"])
            op("dve", lambda e: e.bn_aggr(out=mv[:], in_=stats[:]), reads=["stats0", "stats1"], writes=["mv"])
            op("act", lambda e: e.activation(out=rs[:, 0:1], in_=mv[:, 1:2], func=AF.Ln, bias=epsb[:, 0:1]), reads=["mv", "epsb"], writes=["rs0"])
            op("act", lambda e: e.activation(out=rs[:, 1:2], in_=rs[:, 0:1], func=AF.Exp, scale=-0.5), reads=["rs0"], writes=["rs1"])
            op("dve", lambda e: e.tensor_scalar(out=dst[:], in0=src[:], scalar1=mv[:, 0:1], scalar2=rs[:, 1:2], op0=ALU.subtract, op1=ALU.mult), reads=src_keys + ["mv", "rs1"], writes=dst_keys)
            op("pool", lambda e: e.tensor_tensor(out=dst[:], in0=dst[:], in1=gt[:], op=ALU.mult), reads=dst_keys + [gk], writes=dst_keys)
            op("pool", lambda e: e.tensor_tensor(out=dst[:], in0=dst[:], in1=bt[:], op=ALU.add), reads=dst_keys + [bk], writes=dst_keys)

        epsb = sb("epsb", [128, 1])
        op("dve", lambda e: e.memset(epsb[:], EPS), writes=["epsb"])
        lnst = dict(stats=sb("stats", [128, 2, 6]), mv=sb("mv", [128, 2]), rs=sb("rs", [128, 2]))

        with contextlib.ExitStack() as s0:
            g0 = sb("g0", [128, D], st=s0)
            b0 = sb("b0", [128, D], st=s0)
            xin = [sb(f"xin{i}", [128, D], st=s0) for i in range(2)]
            xo = [sb(f"xo{i}", [128, D], st=s0) for i in range(2)]
            op("sp", lambda e: e.dma_start(out=g0[:], in_=emb_g.partition_broadcast(128)), writes=["g0"], dma="c_g0")
            op("sp", lambda e: e.dma_start(out=b0[:], in_=emb_b.partition_broadcast(128)), writes=["b0"], dma="c_b0")
            for c in range(NT // 128):
                i = c % 2
                op("sp", lambda e, c=c, i=i: e.dma_start(out=xin[i][:], in_=x[c * 128:(c + 1) * 128, :]), writes=[f"xin{i}"], dma=f"xin{i}")
                ln_rows(lnst, xin[i], xo[i], g0, b0, "g0", "b0", "e", [f"xin{i}"], [f"xo{i}"])
                op("sp", lambda e, c=c, i=i: e.dma_start(out=hs0[c * 128:(c + 1) * 128, :], in_=xo[i][:]), reads=[f"xo{i}"], writes=[("hs0", c)], dma=f"xo{i}")
            ust = [sb(f"ust{i}", [128, D], st=s0) for i in range(2)]
            ubf = [sb(f"ubf{i}", [128, D], BF16, st=s0) for i in range(2)]
            utb = [sb(f"utb{i}", [128, 8, 128], BF16, st=s0) for i in range(2)]
            vst = [sb(f"vst{i}", [128, D], st=s0) for i in range(2)]
            vbf = [sb(f"vbf{i}", [128, D], BF16, st=s0) for i in range(2)]
            n = 0
            for l in range(DEPTH):
                for eb in range(NEB):
                    i = n % 2
                    n += 1
                    op("sp", lambda e, l=l, eb=eb, i=i: e.dma_start(out=ust[i][:], in_=peer_u[l, eb * 128:(eb + 1) * 128, :]), writes=[f"ust{i}"], dma=f"ust{i}")
                    op("sp", lambda e, l=l, eb=eb, i=i: e.dma_start(out=vst[i][:], in_=peer_v[l, eb * 128:(eb + 1) * 128, :]), writes=[f"vst{i}"], dma=f"vst{i}")
                    op("act", lambda e, i=i: e.copy(out=ubf[i][:], in_=ust[i][:]), reads=[f"ust{i}"], writes=[f"ubf{i}"])
                    op("pool", lambda e, i=i: e.tensor_copy(out=vbf[i][:], in_=vst[i][:]), reads=[f"vst{i}"], writes=[f"vbf{i}"])
                    for kc in range(8):
                        op("pe", lambda e, i=i, kc=kc: e.transpose(out=PT[:, kc * 128:(kc + 1) * 128], in_=ubf[i][:, kc * 128:(kc + 1) * 128], identity=ident[:]),
                           reads=[f"ubf{i}", "ident"], writes=[("pt", kc)])
                    op("dve", lambda e, i=i: e.tensor_copy(out=utb[i][:], in_=PT[:].rearrange("p (k e) -> p k e", k=8)), reads=[("pt", kc) for kc in range(8)], writes=[f"utb{i}"])
                    op("sp", lambda e, l=l, eb=eb, i=i: e.dma_start(out=UT[l, eb], in_=utb[i][:]), reads=[f"utb{i}"], writes=[("UT", l, eb)], dma=f"utb{i}")
                    op("sp", lambda e, l=l, eb=eb, i=i: e.dma_start(out=VB[l, eb], in_=vbf[i][:]), reads=[f"vbf{i}"], writes=[("VB", l, eb)], dma=f"vbf{i}")
        P.barrier()

        import os
        dbg = os.environ.get("K_DBG", "")
        for l in range(DEPTH):
            if dbg in ("", "mix", "mixpeer"):
                mixer_phase(nc, P, es, locals(), l)
                P.barrier()
            if dbg in ("", "mixpeer"):
                peer_phase(nc, P, es, locals(), l, last=(l == DEPTH - 1))
                P.barrier()
        P.emit()
    return nc


def mixer_phase(nc, P, es_outer, E, l):
    op = P.op
    g = E
    NSEQ, S, NCH = g["NSEQ"], g["S"], g["NCH"]
    PB, PT = g["PB"], g["PT"]
    ident, iof, triu, ones, smask, amask, AMK = g["ident"], g["iof"], g["triu"], g["ones"], g["smask"], g["amask"], g["AMK"]
    hs0, hs1 = g["hs0"], g["hs1"]
    lnst, ln_rows, epsb = g["lnst"], g["ln_rows"], g["epsb"]
    with contextlib.ExitStack() as st:
        def sb(name, shape, dt=F32):
            return st.enter_context(nc.sbuf_tensor(f"{name}_{l}", list(shape), dt))
        Wi = sb("Wi", [128, 8, INC], BF16)
        Wo = sb("Wo", [128, 16, D], BF16)
        Wg = sb("Wg", [128, 4, 2, 128], BF16)
        stg0_ = sb("stg0", [128, 2184]); stg = [stg0_, stg0_]
        l1g, l1b, ng = sb("l1g", [128, D]), sb("l1b", [128, D]), sb("ng", [128, D])
        cw = sb("cw", [128, 16, 5])
        rgc = sb("rgc", [128, 4, 4])
        dtb = sb("dtb", [128, 16]); aneg = sb("aneg", [128, 16]); dsk = sb("dsk", [128, 16]); snk = sb("snk", [128, 8])
        wv = g["w_in"][l].rearrange("(kc p) n -> p kc n", p=128)
        n = 0
        for kc in range(8):
            for hf in range(2):
                i = 0; n += 1
                c0 = hf * 2184
                op("sp", lambda e, kc=kc, c0=c0, i=i: e.dma_start(out=stg[i][:], in_=wv[:, kc, c0:c0 + 2184]), writes=[f"stg{i}"], dma=f"stg{i}")
                if hf == 0:
                    op("act", lambda e, kc=kc, i=i: e.copy(out=Wi[:, kc, 0:2184], in_=stg[i][:]), reads=[f"stg{i}"], writes=[("Wi", kc, 0)])
                else:
                    op("act", lambda e, kc=kc, i=i: e.copy(out=Wi[:, kc, 2184:C_Q], in_=stg[i][:, 0:C_Q - 2184]), reads=[f"stg{i}"], writes=[("Wi", kc, 1)])
                    op("dve", lambda e, kc=kc, i=i: e.tensor_copy(
                        out=Wi[:, kc, C_Q:C_K].rearrange("p (r kv d) -> p kv r d", r=4, kv=2, d=64),
                        in_=stg[i][:, C_Q - 2184:C_K - 2184].rearrange("p (kv r d) -> p kv r d", r=4, kv=2, d=64)), reads=[f"stg{i}"], writes=[("Wi", kc, 2)])
                    op("act", lambda e, kc=kc, i=i: e.copy(out=Wi[:, kc, C_K:INC], in_=stg[i][:, C_K - 2184:INC - 2184]), reads=[f"stg{i}"], writes=[("Wi", kc, 3)])
        WIK = [("Wi", kc, j) for kc in range(8) for j in range(4)]
        wov = g["w_out"][l].rearrange("(kc p) n -> p kc n", p=128)
        for kc in range(16):
            i = 0; n += 1
            op("sp", lambda e, kc=kc, i=i: e.dma_start(out=stg[i][:, 0:D], in_=wov[:, kc, :]), writes=[f"stg{i}"], dma=f"stg{i}")
            op("act", lambda e, kc=kc, i=i: e.copy(out=Wo[:, kc, :], in_=stg[i][:, 0:D]), reads=[f"stg{i}"], writes=[("Wo", kc)])
        WOK = [("Wo", kc) for kc in range(16)]
        op("dve", lambda e: e.memset(wgs[:], 0.0), writes=["wgs"])
        for gi, wsrc in enumerate((g["rg_wa"], g["rg_wx"])):
            for hf in range(2):
                srcv = wsrc[l].rearrange("(b two) c d -> two c b d", two=2)[hf]
                op("sp", lambda e, gi=gi, hf=hf, srcv=srcv: e.dma_start(out=wgs[hf * 64:(hf + 1) * 64, :, gi, hf * 64:(hf + 1) * 64], in_=srcv), reads=[], writes=["wgs"], dma="c_wgs")
        op("dve", lambda e: e.tensor_copy(out=Wg[:], in_=wgs[:]), reads=["wgs"], writes=["Wg"])
        for t_, src_, k_ in ((l1g, g["ln1_g"], "l1g"), (l1b, g["ln1_b"], "l1b"), (ng, g["ssm_norm_g"], "ng")):
            op("sp", lambda e, t_=t_, src_=src_: e.dma_start(out=t_[:], in_=src_[l].partition_broadcast(128)), writes=[k_], dma="c_" + k_)
        for t_, src_, k_ in ((dtb, g["ssm_dt_bias"], "dtb"), (aneg, g["ssm_a_log"], "aneg"), (dsk, g["ssm_d"], "dsk"), (snk, g["attn_sinks"], "snk")):
            op("sp", lambda e, t_=t_, src_=src_: e.dma_start(out=t_[:], in_=src_[l].partition_broadcast(128)), writes=[k_], dma="c_" + k_)
        for k_ in range(4):
            op("sp", lambda e, k_=k_: e.dma_start(out=cw[:, 0:4, k_:k_ + 1], in_=g["rg_conv_w"][l, k_].rearrange("(b p o) -> p b o", p=128, o=1), allow_slow_non_contiguous=True), writes=["cw"], dma="c_cw")
            op("sp", lambda e, k_=k_: e.dma_start(out=cw[:, 4:16, k_:k_ + 1], in_=g["ssm_conv_w"][l, k_].rearrange("(b p o) -> p b o", p=128, o=1), allow_slow_non_contiguous=True), writes=["cw"], dma="c_cw")
        op("sp", lambda e: e.dma_start(out=cw[:, 0:4, 4:5], in_=g["rg_conv_b"][l].rearrange("(b p o) -> p b o", p=128, o=1), allow_slow_non_contiguous=True), writes=["cw"], dma="c_cw")
        op("sp", lambda e: e.dma_start(out=cw[:, 4:16, 4:5], in_=g["ssm_conv_b"][l].rearrange("(b p o) -> p b o", p=128, o=1), allow_slow_non_contiguous=True), writes=["cw"], dma="c_cw")
        for j, src_ in enumerate((g["rg_ba"], g["rg_bx"], g["rg_lambda"])):
            op("sp", lambda e, j=j, src_=src_: e.dma_start(out=rgc[:, :, j:j + 1], in_=src_[l].rearrange("(b p o) -> p b o", p=128, o=1), allow_slow_non_contiguous=True), writes=["rgc"], dma="c_rgc")
        op("dve", lambda e: e.tensor_scalar(out=cw[:, 4:16, :], in0=cw[:, 4:16, :], scalar1=0.5, scalar2=None, op0=ALU.mult), reads=["cw"], writes=["cw"])
        op("act", lambda e: e.activation(out=rgc[:, :, 3], in_=rgc[:, :, 2], func=AF.Exp, scale=-1.0), reads=["rgc"], writes=["rgc3"])
        op("act", lambda e: e.activation(out=rgc[:, :, 3], in_=rgc[:, :, 3], func=AF.Ln, bias=1.0), reads=["rgc3"], writes=["rgc3"])
        op("dve", lambda e: e.tensor_scalar(out=rgc[:, :, 2], in0=rgc[:, :, 3], scalar1=-4.0, scalar2=None, op0=ALU.mult), reads=["rgc3", "rgc"], writes=["rgc"])
        op("dve", lambda e: e.tensor_scalar(out=rgc[:, :, 0:2], in0=rgc[:, :, 0:2], scalar1=0.5, scalar2=None, op0=ALU.mult), reads=["rgc"], writes=["rgc"])
        op("act", lambda e: e.activation(out=aneg[:], in_=aneg[:], func=AF.Exp), reads=["aneg"], writes=["aneg"])
        op("dve", lambda e: e.tensor_scalar(out=aneg[:], in0=aneg[:], scalar1=-1.0, scalar2=None, op0=ALU.mult), reads=["aneg"], writes=["aneg"])
        op("pool", lambda e: e.tensor_scalar(out=ng[:], in0=ng[:], scalar1=0.5, scalar2=None, op0=ALU.mult), reads=["ng"], writes=["ng"])
        h = sb("h", [128, D]); hb = sb("hb", [128, D], BF16); yb = hb; hT = sb("hT", [128, 8, 128], BF16)
        X = sb("X", [128, 16, 131]); xc = sb("xc", [128, 16, 128]); wgs = xc[:, 0:8, :].rearrange("p (a b) t -> p a b t", a=4)
        gate = sb("gate", [128, 4, 128]); qT = sb("qT", [128, 4, 128], BF16)
        kT = sb("kT", [128, 2, 128], BF16); vr = sb("vr", [128, 2, 128], BF16)
        gz = sb("gz", [128, D]); dt = sb("dt", [128, 16]); sm = sb("sm", [128, 16, 8])
        ymT = sb("ymT", [128, 16, 128], BF16)
        lm = sb("lm", [128, 256]); pb = sb("pb", [128, 256], BF16); pTt = sb("pTt", [128, 2, 128], BF16)
        yc = sb("yc", [128, 512], BF16)
        T1KEYS = ["tA", "tB", "tC", "tD", "tE", "tF", "tG"]
        xcb = sb("xcb", [128, 128], BF16); lru = sb("lru", [128, 4, 128]); lst = sb("lst", [128, 4])
        xsb = sb("xsb", [128, 12, 128], BF16)
        xtm = sb("xtm", [128, D], BF16); xdt = sb("xdt", [128, D], BF16); xds = sb("xds", [128, D], BF16); btm = sb("btm", [128, 2, 128], BF16)
        daU = sb("daU", [128, 8, 128]); seg = sb("seg", [128, 8, 128]); t1 = [seg[:, i, :] for i in range(8)]; MT = sb("MT", [128, 8, 128], BF16)
        ssf = sb("ssf", [128, D]); ssb = sb("ssb", [128, D], BF16)
        y1 = sb("y1", [128, D]); y2 = sb("y2", [128, D])
        r1 = y2; ho = y1; th = sb("th", [128, 128]); rn = sb("rn", [128, 4])
        s16 = lambda i: sm[:, :, i]
        PTv = PT[:].rearrange("p (k e) -> p k e", k=8)
        PT1 = PB[1][:].bitcast(BF16)
        PT1v = PT1.rearrange("p (k e) -> p k e", k=8)

        for sq in range(NSEQ):
            op("dve", lambda e: e.memset(X[:, :, 128:131], 0.0), writes=["Xhist"])
            op("dve", lambda e: e.memset(lst[:], 0.0), writes=["lst"])
            op("dve", lambda e: e.memset(ssf[:], 0.0), writes=["ssf"])
            op("dve", lambda e: e.memset(ssb[:], 0.0), writes=["ssb"])
            op("dve", lambda e: e.memset(kT[:, 1, :], 0.0), writes=["kT1"])
            op("dve", lambda e: e.memset(vr[:, 1, :], 0.0), writes=["vr1"])
            for c in range(NCH):
                ci = sq * NCH + c
                rows = slice(ci * 128, (ci + 1) * 128)
                slot, pslot = c % 2, (c + 1) % 2
                op("sp", lambda e, rows=rows: e.dma_start(out=h[:], in_=hs0[rows, :]), reads=[("hs0", ci)], writes=["h"], dma="h")
                op("act", lambda e: e.copy(out=hb[:], in_=h[:]), reads=["h"], writes=["hb", ("yb", 0), ("yb", 1)])
                for kc in range(8):
                    op("pe", lambda e, kc=kc: e.transpose(out=PT[:, kc * 128:(kc + 1) * 128], in_=hb[:, kc * 128:(kc + 1) * 128], identity=ident[:]), reads=["hb", "ident"], writes=[("pt", kc)])
                op("dve", lambda e: e.tensor_copy(out=hT[:], in_=PTv), reads=[("pt", k) for k in range(8)], writes=["hT"])
                op("pool", lambda e: e.tensor_copy(out=X[:, :, 0:3], in_=X[:, :, 128:131]), reads=["Xhist", "X"], writes=["Xh0"])

                def fm_group(cols, bank, nblk):
                    for b, c0 in enumerate(cols):
                        for kc in range(8):
                            op("pe", lambda e, b=b, c0=c0, kc=kc: e.matmul(PB[bank][:, b * 128:(b + 1) * 128], lhsT=Wi[:, kc, c0:c0 + 128], rhs=hT[:, kc, :], start=(kc == 0), stop=(kc == 7)),
                               reads=["hT"] + WIK, writes=[("pb", bank, b)])
                    return [("pb", bank, b) for b in range(nblk)]
                ks = fm_group([C_RGX + 128 * b for b in range(4)], 0, 4)
                op("act", lambda e: e.copy(out=X[:, 0:4, 3:131], in_=PB[0][:].rearrange("p (b t) -> p b t", b=4)), reads=ks + ["Xh0"], writes=["X", "Xhist"])
                ks = fm_group([C_RGG + 128 * b for b in range(4)], 1, 4)
                op("act", lambda e: e.copy(out=gate[:], in_=PB[1][:].rearrange("p (b t) -> p b t", b=4)), reads=ks, writes=["gate"])
                for gq in range(3):
                    ks = fm_group([C_XBC + 128 * (4 * gq + b) for b in range(4)], gq % 2, 4)
                    op("act", lambda e, gq=gq: e.copy(out=X[:, 4 + 4 * gq:8 + 4 * gq, 3:131], in_=PB[gq % 2][:].rearrange("p (b t) -> p b t", b=4)), reads=ks + ["Xh0"], writes=["X", "Xhist"])
                ks = fm_group([C_Q + 128 * b for b in range(4)], 1, 4)
                op("act", lambda e: e.copy(out=qT[:], in_=PB[1][:].rearrange("p (b t) -> p b t", b=4)), reads=ks, writes=["qT"])
                ks = fm_group([C_K], 0, 1)
                op("act", lambda e, slot=slot: e.copy(out=kT[:, slot, :], in_=PB[0][:, 0:128]), reads=ks, writes=[f"kT{slot}"])
                for hf in range(2):
                    for kc in range(8):
                        op("pe", lambda e, hf=hf, kc=kc: e.matmul(PB[2 + hf][:, :], lhsT=hT[:, kc, :], rhs=Wi[:, kc, C_Z + 512 * hf:C_Z + 512 * (hf + 1)], start=(kc == 0), stop=(kc == 7)),
                           reads=["hT"] + WIK, writes=[("pbz", hf)])
                for kc in range(8):
                    op("pe", lambda e, kc=kc: e.matmul(PB[4][:, 0:128], lhsT=hT[:, kc, :], rhs=Wi[:, kc, C_V:C_V + 128], start=(kc == 0), stop=(kc == 7)), reads=["hT"] + WIK, writes=["pbv"])
                for kc in range(8):
                    op("pe", lambda e, kc=kc: e.matmul(PB[4][:, 128:144], lhsT=hT[:, kc, :], rhs=Wi[:, kc, C_DT:C_DT + 16], start=(kc == 0), stop=(kc == 7)), reads=["hT"] + WIK, writes=["pbdt"])
                op("act", lambda e, slot=slot: e.copy(out=vr[:, slot, :], in_=PB[4][:, 0:128]), reads=["pbv"], writes=[f"vr{slot}"])
                for hf in range(2):
                    op("act", lambda e, hf=hf: e.activation(out=gz[:, 512 * hf:512 * (hf + 1)], in_=PB[2 + hf][:, :], func=AF.Tanh, scale=0.5), reads=[("pbz", hf)], writes=[("gz", hf)])
                    op("dve", lambda e, hf=hf: e.scalar_tensor_tensor(out=gz[:, 512 * hf:512 * (hf + 1)], in0=gz[:, 512 * hf:512 * (hf + 1)], scalar=1.0, in1=PB[2 + hf][:, :], op0=ALU.add, op1=ALU.mult),
                       reads=[("gz", hf), ("pbz", hf)], writes=[("gz", hf)])
                op("dve", lambda e: e.tensor_tensor(out=s16(0), in0=PB[4][:, 128:144], in1=dtb[:], op=ALU.add), reads=["pbdt", "dtb"], writes=["s0"])
                op("act", lambda e: e.activation(out=s16(0), in_=s16(0), func=AF.Exp), reads=["s0"], writes=["s0"])
                op("act", lambda e: e.activation(out=dt[:], in_=s16(0), func=AF.Ln, bias=1.0), reads=["s0"], writes=["dt"])
                op("dve", lambda e: e.tensor_tensor(out=s16(1), in0=dt[:], in1=aneg[:], op=ALU.mult), reads=["dt", "aneg"], writes=["da"])

                _chA = []
                P.chain = _chA
                for hd in range(8):
                    r_, kv = hd % 4, hd // 4
                    pr = slice(kv * 64, kv * 64 + 64)
                    bk = 5 + hd % 2
                    op("pe", lambda e, r_=r_, pr=pr, bk=bk, pslot=pslot: e.matmul(PB[bk][:, 0:128], lhsT=qT[pr, r_, :], rhs=kT[pr, pslot, :], start=True, stop=True), reads=["qT", f"kT{pslot}"], writes=[("lg", bk, 0)])
                    op("pe", lambda e, r_=r_, pr=pr, bk=bk, slot=slot: e.matmul(PB[bk][:, 128:256], lhsT=qT[pr, r_, :], rhs=kT[pr, slot, :], start=True, stop=True), reads=["qT", f"kT{slot}"], writes=[("lg", bk, 1)])
                    mv_ = 0 if c == 0 else 1
                    op("dve", lambda e, bk=bk, mv_=mv_: e.tensor_tensor(out=lm[:], in0=PB[bk][:, 0:256], in1=amask[:, mv_, :], op=ALU.add), reads=[("lg", bk, 0), ("lg", bk, 1)] + AMK, writes=["lm"])
                    op("dve", lambda e: e.reduce_max(out=s16(2)[:, 0:1], in_=lm[:], axis=AX.X), reads=["lm"], writes=["amx"])
                    op("dve", lambda e, hd=hd: e.tensor_scalar(out=s16(2)[:, 1:2], in0=s16(2)[:, 0:1], scalar1=0.125, scalar2=snk[:, hd:hd + 1], op0=ALU.mult, op1=ALU.max), reads=["amx", "snk"], writes=["am"])
                    op("dve", lambda e: e.tensor_scalar(out=s16(2)[:, 2:3], in0=s16(2)[:, 1:2], scalar1=-1.0, scalar2=None, op0=ALU.mult), reads=["am"], writes=["anm"])
                    op("dve", lambda e: e.memset(s16(2)[:, 3:4], 0.0), writes=["asum"])
                    op("act", lambda e: e.activation(out=pb[:], in_=lm[:], func=AF.Exp, bias=s16(2)[:, 2:3], scale=0.125, accum_out=s16(2)[:, 3:4]), reads=["lm", "anm", "asum"], writes=["pb", "asum"])
                    op("act", lambda e, hd=hd: e.activation(out=s16(2)[:, 4:5], in_=snk[:, hd:hd + 1], func=AF.Exp, bias=s16(2)[:, 2:3]), reads=["snk", "anm"], writes=["asnk"])
                    op("dve", lambda e: e.tensor_tensor(out=s16(2)[:, 5:6], in0=s16(2)[:, 3:4], in1=s16(2)[:, 4:5], op=ALU.add), reads=["asum", "asnk"], writes=["aden"])
                    op("dve", lambda e: e.reciprocal(out=s16(2)[:, 6:7], in_=s16(2)[:, 5:6]), reads=["aden"], writes=["arec"])
                    for j in range(2):
                        op("pe", lambda e, j=j: e.transpose(out=PT[:, j * 128:(j + 1) * 128], in_=pb[:, j * 128:(j + 1) * 128], identity=ident[:]), reads=["pb", "ident"], writes=[("pt", j)])
                    op("act", lambda e: e.copy(out=pTt[:], in_=PTv[:, 0:2, :]), reads=[("pt", 0), ("pt", 1)], writes=["pTt"])
                    osl = slice(hd * 64, hd * 64 + 64)
                    vs = slice(kv * 64, kv * 64 + 64)
                    op("pe", lambda e, osl=osl, vs=vs, pslot=pslot: e.matmul(PB[4][:, osl], lhsT=pTt[:, 0, :], rhs=vr[:, pslot, vs], start=True, stop=False), reads=["pTt", f"vr{pslot}"], writes=[("pv", hd)])
                    op("pe", lambda e, osl=osl, vs=vs, slot=slot: e.matmul(PB[4][:, osl], lhsT=pTt[:, 1, :], rhs=vr[:, slot, vs], start=False, stop=True), reads=["pTt", f"vr{slot}", ("pv", hd)], writes=[("pv", hd)])
                    op("dve", lambda e, osl=osl: e.tensor_scalar(out=yc[:, osl], in0=PB[4][:, osl], scalar1=s16(2)[:, 6:7], scalar2=None, op0=ALU.mult), reads=[("pv", hd), "arec", "pbdt", "pbv"], writes=[("yc", hd)])
                for j in range(4):
                    op("pe", lambda e, j=j: e.transpose(out=PT[:, j * 128:(j + 1) * 128], in_=yc[:, j * 128:(j + 1) * 128], identity=ident[:]), reads=[("yc", hd) for hd in range(8)] + ["ident"], writes=[("pt", j)])
                op("act", lambda e: e.copy(out=ymT[:, 12:16, :], in_=PTv[:, 0:4, :]), reads=[("pt", j) for j in range(4)], writes=["ymT_c"])
                if c == 0:
                    pass

                _chB = []
                P.chain = _chB
                for b in range(16):
                    eng = "dve"
                    op(eng, lambda e, b=b: e.tensor_scalar(out=xc[:, b, :], in0=X[:, b, 0:128], scalar1=cw[:, b, 0:1], scalar2=cw[:, b, 4:5], op0=ALU.mult, op1=ALU.add), reads=["X", "Xh0", "cw"], writes=[("xc", b), "wgs"])
                    for k in range(1, 4):
                        op(eng, lambda e, b=b, k=k: e.scalar_tensor_tensor(out=xc[:, b, :], in0=X[:, b, k:k + 128], scalar=cw[:, b, k:k + 1], in1=xc[:, b, :], op0=ALU.mult, op1=ALU.add), reads=["X", "Xh0", "cw", ("xc", b)], writes=[("xc", b)])

                for b in range(4):
                    A, B_, C_, D_, E_, F_, G_, H_ = [t1[i] for i in range(8)]
                    op("act", lambda e, b=b: e.copy(out=xcb[:], in_=xc[:, b, :]), reads=[("xc", b)], writes=["xcb"])
                    bk = 0
                    for gi in range(2):
                        op("pe", lambda e, b=b, gi=gi, bk=bk: e.matmul(PB[bk][:, gi * 128:(gi + 1) * 128], lhsT=Wg[:, b, gi, :], rhs=xcb[:], start=True, stop=True), reads=["Wg", "xcb"], writes=[("pb", 0, gi)])
                    op("act", lambda e, b=b, bk=bk: e.activation(out=A[:], in_=PB[bk][:, 0:128], func=AF.Tanh, bias=rgc[:, b, 0:1], scale=0.5), reads=[("pb", 0, 0), "rgc"], writes=["tA", ("seg", 0), ("seg", 1)])
                    op("act", lambda e, b=b, bk=bk: e.activation(out=B_[:], in_=PB[bk][:, 128:256], func=AF.Tanh, bias=rgc[:, b, 1:2], scale=0.5), reads=[("pb", 0, 1), "rgc"], writes=["tB", ("seg", 0), ("seg", 1)])
                    op("dve", lambda e, b=b: e.tensor_scalar(out=C_[:], in0=A[:], scalar1=1.0, scalar2=rgc[:, b, 2:3], op0=ALU.add, op1=ALU.mult), reads=["tA", "rgc"], writes=["tC"])
                    op("act", lambda e: e.activation(out=D_[:], in_=C_[:], func=AF.Exp), reads=["tC"], writes=["tD"])
                    op("dve", lambda e: e.tensor_scalar(out=E_[:], in0=C_[:], scalar1=2.0 / 6.0, scalar2=1.0, op0=ALU.mult, op1=ALU.add), reads=["tC"], writes=["tE"])
                    for dv in (5.0, 4.0, 3.0, 2.0):
                        op("dve", lambda e, dv=dv: e.scalar_tensor_tensor(out=E_[:], in0=C_[:], scalar=2.0 / dv, in1=E_[:], op0=ALU.mult, op1=ALU.mult), reads=["tC", "tE"], writes=["tE"])
                        op("dve", lambda e: e.tensor_scalar(out=E_[:], in0=E_[:], scalar1=1.0, scalar2=None, op0=ALU.add), reads=["tE"], writes=["tE"])
                    op("dve", lambda e: e.scalar_tensor_tensor(out=E_[:], in0=C_[:], scalar=-2.0, in1=E_[:], op0=ALU.mult, op1=ALU.mult), reads=["tC", "tE"], writes=["tE"])
                    op("dve", lambda e: e.tensor_scalar(out=E_[:], in0=E_[:], scalar1=1e-30, scalar2=None, op0=ALU.max), reads=["tE"], writes=["tE"])
                    op("act", lambda e: e.activation(out=E_[:], in_=E_[:], func=AF.Ln), reads=["tE"], writes=["tE"])
                    op("act", lambda e: e.activation(out=E_[:], in_=E_[:], func=AF.Exp, scale=0.5), reads=["tE"], writes=["tE"])
                    op("dve", lambda e, b=b: e.scalar_tensor_tensor(out=F_[:], in0=B_[:], scalar=1.0, in1=xc[:, b, :], op0=ALU.add, op1=ALU.mult), reads=["tB", ("xc", b)], writes=["tF"])
                    op("dve", lambda e: e.scalar_tensor_tensor(out=F_[:], in0=F_[:], scalar=0.5, in1=E_[:], op0=ALU.mult, op1=ALU.mult), reads=["tF", "tE"], writes=["tF"])
                    op("dve", lambda e, b=b: e.tensor_tensor_scan(out=lru[:, b, :], data0=D_[:], data1=F_[:], initial=lst[:, b:b + 1], op0=ALU.mult, op1=ALU.add), reads=["tD", "tF", "lst"], writes=[("lru", b)])
                    op("dve", lambda e, b=b: e.tensor_copy(out=lst[:, b:b + 1], in_=lru[:, b, 127:128]), reads=[("lru", b)], writes=["lst"])
                    op("pool", lambda e, b=b: e.tensor_tensor(out=G_[:], in0=gate[:, b, :], in1=gate[:, b, :], op=ALU.mult), reads=["gate"], writes=["tG"])
                    op("pool", lambda e: e.tensor_scalar(out=G_[:], in0=G_[:], scalar1=0.044715, scalar2=1.0, op0=ALU.mult, op1=ALU.add), reads=["tG"], writes=["tG"])
                    op("pool", lambda e, b=b: e.tensor_tensor(out=G_[:], in0=G_[:], in1=gate[:, b, :], op=ALU.mult), reads=["tG", "gate"], writes=["tG"])
                    op("act", lambda e: e.activation(out=G_[:], in_=G_[:], func=AF.Tanh, scale=0.7978845608028654), reads=["tG"], writes=["tG"])
                    op("dve", lambda e, b=b: e.scalar_tensor_tensor(out=G_[:], in0=G_[:], scalar=1.0, in1=gate[:, b, :], op0=ALU.add, op1=ALU.mult), reads=["tG", "gate"], writes=["tG"])
                    op("dve", lambda e, b=b: e.scalar_tensor_tensor(out=ymT[:, b, :], in0=G_[:], scalar=0.5, in1=lru[:, b, :], op0=ALU.mult, op1=ALU.mult), reads=["tG", ("lru", b)], writes=[("ymT_a", b)])

                for b in range(12):
                    op("act", lambda e, b=b: e.activation(out=th[:], in_=xc[:, 4 + b, :], func=AF.Tanh), reads=[("xc", 4 + b)], writes=["th"])
                    op("dve", lambda e, b=b: e.scalar_tensor_tensor(out=xsb[:, b, :], in0=th[:], scalar=1.0, in1=xc[:, 4 + b, :], op0=ALU.add, op1=ALU.mult), reads=["th", ("xc", 4 + b)], writes=[("xsb", b)])
                for kc in range(8):
                    op("pe", lambda e, kc=kc: e.transpose(out=PT1[:, kc * 128:(kc + 1) * 128], in_=xsb[:, kc, :], identity=ident[:]), reads=[("xsb", kc), "ident"], writes=[("pb", 1, kc)])
                op("act", lambda e: e.copy(out=xtm[:], in_=PT1), reads=[("pb", 1, k) for k in range(8)], writes=["xtm"])
                for j in range(2):
                    op("pe", lambda e, j=j: e.transpose(out=PT1[:, j * 128:(j + 1) * 128], in_=xsb[:, 8 + j, :], identity=ident[:]), reads=[("xsb", 8 + j), "ident"], writes=[("pb", 1, j)])
                op("act", lambda e: e.copy(out=btm[:], in_=PT1v[:, 0:2, :]), reads=[("pb", 1, 0), ("pb", 1, 1)], writes=["btm"])
                op("pe", lambda e: e.matmul(PB[3][:, 0:16], lhsT=triu[:], rhs=s16(1), start=True, stop=True), reads=["triu", "da"], writes=[("pbz", 1)])
                op("pe", lambda e: e.matmul(PB[3][:, 16:32], lhsT=ones[:], rhs=s16(1), start=True, stop=True), reads=["ones", "da"], writes=[("pbz", 1)])
                op("dve", lambda e: e.tensor_copy(out=s16(3), in_=PB[3][:, 0:16]), reads=[("pbz", 1)], writes=["cs"])
                op("dve", lambda e: e.tensor_copy(out=s16(4), in_=PB[3][:, 16:32]), reads=[("pbz", 1)], writes=["tot"])
                op("act", lambda e: e.activation(out=s16(5), in_=s16(3), func=AF.Exp), reads=["cs"], writes=["ea"])
                op("dve", lambda e: e.tensor_tensor(out=s16(6), in0=s16(4), in1=s16(3), op=ALU.subtract), reads=["cs", "tot"], writes=["ds"])
                op("act", lambda e: e.activation(out=s16(6), in_=s16(6), func=AF.Exp), reads=["ds"], writes=["ds"])
                op("act", lambda e: e.activation(out=s16(7), in_=s16(4), func=AF.Exp), reads=["tot"], writes=["cd"])
                op("dve", lambda e: e.tensor_tensor(out=s16(6), in0=s16(6), in1=dt[:], op=ALU.mult), reads=["ds", "dt"], writes=["ds"])
                x3 = xtm[:].rearrange("p (h d) -> p h d", h=16)
                op("dve", lambda e: e.tensor_tensor(out=xdt[:].rearrange("p (h d) -> p h d", h=16), in0=x3, in1=bc(dt[:].unsqueeze(2), [128, 16, 64]), op=ALU.mult), reads=["xtm", "dt"], writes=["xdt"])
                op("pool", lambda e: e.tensor_tensor(out=xds[:].rearrange("p (h d) -> p h d", h=16), in0=x3, in1=bc(s16(6).unsqueeze(2), [128, 16, 64]), op=ALU.mult), reads=["xtm", "ds"], writes=["xds"])
                for g_ in range(2):
                    hs_ = slice(8 * g_, 8 * g_ + 8)
                    cols = slice(512 * g_, 512 * g_ + 512)
                    op("dve", lambda e, hs_=hs_: e.tensor_tensor(out=daU[:], in0=bc(triu[:].unsqueeze(1), [128, 8, 128]), in1=bc(s16(1)[:, hs_].unsqueeze(2), [128, 8, 128]), op=ALU.mult), reads=["triu", "da"], writes=["daU"])
                    for j in range(2):
                        op("pe", lambda e, j=j: e.matmul(PB[2 + j][:, :], lhsT=ones[:], rhs=daU[:, 4 * j:4 * j + 4, :].rearrange("p a b -> p (a b)"), start=True, stop=True), reads=["ones", "daU"], writes=[("pbz", j)])
                        op("dve", lambda e, j=j, g_=g_: e.tensor_tensor(out=seg[:, 4 * j:4 * j + 4, :], in0=PB[2 + j][:, :].rearrange("p (a b) -> p a b", a=4),
                                                                  in1=bc(s16(3)[:, 8 * g_ + 4 * j:8 * g_ + 4 * j + 4].unsqueeze(2), [128, 4, 128]), op=ALU.subtract), reads=[("pbz", j), "cs"], writes=[("seg", j)] + T1KEYS)
                    op("pool", lambda e: e.tensor_tensor(out=seg[:], in0=seg[:], in1=bc(smask[:].unsqueeze(1), [128, 8, 128]), op=ALU.add), reads=[("seg", 0), ("seg", 1), "smask"], writes=[("seg", 0), ("seg", 1)])
                    op("act", lambda e: e.activation(out=seg[:], in_=seg[:], func=AF.Exp), reads=[("seg", 0), ("seg", 1)], writes=[("seg", 0), ("seg", 1)])
                    op("pe", lambda e, g_=g_: e.matmul(PB[0][:, 0:128], lhsT=xsb[:, 8 + g_, :], rhs=xsb[:, 10 + g_, :], start=True, stop=True), reads=[("xsb", 8 + g_), ("xsb", 10 + g_)], writes=[("pb", 0, 0)])
                    op("dve", lambda e: e.tensor_tensor(out=MT[:], in0=seg[:], in1=bc(PB[0][:, 0:128].unsqueeze(1), [128, 8, 128]), op=ALU.mult), reads=[("seg", 0), ("seg", 1), ("pb", 0, 0)], writes=["MT"])
                    for hh in range(8):
                        hcol = slice((8 * g_ + hh) * 64, (8 * g_ + hh) * 64 + 64)
                        op("pe", lambda e, hh=hh, hcol=hcol, g_=g_: e.matmul(PB[1][:, hh * 64:(hh + 1) * 64], lhsT=MT[:, hh, :], rhs=xdt[:, hcol], start=True, stop=True),
                           reads=["MT", "xdt"], writes=[("pb", 1, hh // 2)])
                    yo = 2
                    op("pe", lambda e, g_=g_, yo=yo, cols=cols: e.matmul(PB[yo][:, :], lhsT=xsb[:, 10 + g_, :], rhs=ssb[:, cols], start=True, stop=True),
                       reads=[("xsb", 10 + g_), "ssb"], writes=[("pbz", 0)])
                    op("pe", lambda e, g_=g_, cols=cols: e.matmul(PB[3][:, :], lhsT=btm[:, g_, :], rhs=xds[:, cols], start=True, stop=True), reads=["btm", "xds"], writes=[("pbz", 1)])
                    sv = ssf[:, cols].rearrange("p (h d) -> p h d", h=8)
                    op("pool", lambda e, sv=sv, hs_=hs_: e.tensor_tensor(out=sv, in0=sv, in1=bc(s16(7)[:, hs_].unsqueeze(2), [128, 8, 64]), op=ALU.mult), reads=["ssf", "cd"], writes=["ssf"])
                    op("dve", lambda e, cols=cols: e.tensor_tensor(out=ssf[:, cols], in0=ssf[:, cols], in1=PB[3][:, :], op=ALU.add), reads=["ssf", ("pbz", 1)], writes=["ssf"])
                    yv = y1[:, cols].rearrange("p (h d) -> p h d", h=8)
                    op("dve", lambda e, yv=yv, yo=yo, hs_=hs_: e.tensor_tensor(out=yv, in0=PB[yo][:, :].rearrange("p (h d) -> p h d", h=8), in1=bc(s16(5)[:, hs_].unsqueeze(2), [128, 8, 64]), op=ALU.mult), reads=[("pbz", 0), "ea"], writes=[("y1", g_), "ho"])
                    op("act", lambda e, g_=g_, cols=cols: e.copy(out=ssb[:, cols], in_=ssf[:, cols]), reads=["ssf", ("pbz", 0)], writes=["ssb"])
                    op("dve", lambda e, g_=g_, cols=cols: e.tensor_tensor(out=y1[:, cols], in0=y1[:, cols], in1=PB[1][:, :], op=ALU.add), reads=[("y1", g_)] + [("pb", 1, b) for b in range(4)], writes=[("y1", g_)])
                    op("pool", lambda e, cols=cols, hs_=hs_: e.tensor_tensor(out=y2[:, cols].rearrange("p (h d) -> p h d", h=8), in0=xtm[:, cols].rearrange("p (h d) -> p h d", h=8), in1=bc(dsk[:, hs_].unsqueeze(2), [128, 8, 64]), op=ALU.mult), reads=["xtm", "dsk"], writes=[("y2", g_), ("r1", 0), ("r1", 1)])
                    op("pool", lambda e, g_=g_, cols=cols: e.tensor_tensor(out=y1[:, cols], in0=y1[:, cols], in1=y2[:, cols], op=ALU.add), reads=[("y1", g_), ("y2", g_)], writes=[("y1", g_)])
                    op("pool", lambda e, g_=g_, cols=cols: e.tensor_tensor(out=y1[:, cols], in0=y1[:, cols], in1=gz[:, cols], op=ALU.mult), reads=[("y1", g_), ("gz", g_)], writes=[("y1", g_)])
                    op("dve", lambda e, g_=g_: e.memset(rn[:, g_:g_ + 1], 0.0), writes=[("rn", g_)])
                    op("act", lambda e, g_=g_, cols=cols: e.activation(out=y2[:, cols], in_=y1[:, cols], func=AF.Square, accum_out=rn[:, g_:g_ + 1]), reads=[("y1", g_), ("rn", g_), ("y2", g_)], writes=[("y2", g_), ("rn", g_)])
                    op("dve", lambda e, g_=g_: e.tensor_scalar(out=rn[:, g_:g_ + 1], in0=rn[:, g_:g_ + 1], scalar1=0.25 / 512.0, scalar2=EPS, op0=ALU.mult, op1=ALU.add), reads=[("rn", g_)], writes=[("rn", g_)])
                    op("act", lambda e, g_=g_: e.activation(out=rn[:, g_:g_ + 1], in_=rn[:, g_:g_ + 1], func=AF.Ln), reads=[("rn", g_)], writes=[("rn", g_)])
                    op("act", lambda e, g_=g_: e.activation(out=rn[:, g_:g_ + 1], in_=rn[:, g_:g_ + 1], func=AF.Exp, scale=-0.5), reads=[("rn", g_)], writes=[("rn", g_)])
                    op("dve", lambda e, g_=g_, cols=cols: e.scalar_tensor_tensor(out=yb[:, cols], in0=y1[:, cols], scalar=rn[:, g_:g_ + 1], in1=ng[:, cols], op0=ALU.mult, op1=ALU.mult), reads=[("y1", g_), ("rn", g_), "ng"], writes=[("yb", g_), "hb"])
                for kc in range(8):
                    op("pe", lambda e, kc=kc: e.transpose(out=PT1[:, kc * 128:(kc + 1) * 128], in_=yb[:, kc * 128:(kc + 1) * 128], identity=ident[:]), reads=[("yb", 0), ("yb", 1), "ident"], writes=[("pb", 1, kc)])
                op("act", lambda e: e.copy(out=ymT[:, 4:12, :], in_=PT1v), reads=[("pb", 1, k) for k in range(8)], writes=["ymT_b"])
                P.chain = None
                P.merge(_chA, _chB)
                YK = ["ymT_b", "ymT_c"] + [("ymT_a", b) for b in range(4)]
                for hf in range(2):
                    for kc in range(16):
                        op("pe", lambda e, hf=hf, kc=kc: e.matmul(PB[hf][:, :], lhsT=ymT[:, kc, :], rhs=Wo[:, kc, 512 * hf:512 * (hf + 1)], start=(kc == 0), stop=(kc == 15)),
                           reads=YK + WOK, writes=[("pb", hf, b) for b in range(4)])
                    op("dve", lambda e, hf=hf: e.scalar_tensor_tensor(out=r1[:, 512 * hf:512 * (hf + 1)], in0=h[:, 512 * hf:512 * (hf + 1)], scalar=ALPHA, in1=PB[hf][:, :], op0=ALU.mult, op1=ALU.add),
                       reads=["h"] + [("pb", hf, b) for b in range(4)], writes=[("r1", hf), ("y2", 0), ("y2", 1)])
                ln_rows(lnst, r1, ho, l1g, l1b, "l1g", "l1b", "m", [("r1", 0), ("r1", 1)], ["ho", ("y1", 0), ("y1", 1)])
                op("sp", lambda e, rows=rows: e.dma_start(out=hs1[rows, :], in_=ho[:]), reads=["ho"], writes=[("hs1", ci)], dma="ho")


def peer_phase(nc, P, es_outer, E, l, last):
    op = P.op
    g = E
    NSEQ, S, NT, TT, NEB = g["NSEQ"], g["S"], g["NT"], g["TT"], g["NEB"]
    PB, PT = g["PB"], g["PT"]
    ident, iocb, io16 = g["ident"], g["iocb"], g["io16"]
    hs0, hs1, out, UT, VB = g["hs0"], g["hs1"], g["out"], g["UT"], g["VB"]
    lnst, ln_rows = g["lnst"], g["ln_rows"]
    dst = out if last else hs0
    G = 4
    NSUB = TT // 128
    PTv = PT[:].rearrange("p (k e) -> p k e", k=8)
    with contextlib.ExitStack() as st:
        def sb(name, shape, dt=F32):
            return st.enter_context(nc.sbuf_tensor(f"{name}_p{l}", list(shape), dt))
        Wq = sb("Wq", [128, 8, D], BF16)
        stq_ = sb("stq0", [128, D]); stg = [stq_, stq_]
        kst = sb("kst", [128, 256]); KT = sb("KT", [128, 256], BF16)
        l2g, l2b = sb("l2g", [128, D]), sb("l2b", [128, D])
        WT = sb("WT", [128, 128, TT], BF16)
        h1T = [sb(f"h1T{i}", [128, 8, TT], BF16) for i in range(2)]
        hq = sb("hq", [128, D]); hqb = sb("hqb", [128, D], BF16)
        qTs = sb("qTs", [128, 8, 128], BF16)
        sc = sb("sc", [128, 16, 128]); scr = sb("scr", [128, 256])
        V16 = sb("V16", [128, 16, 16]); I16 = sb("I16", [128, 16, 16], U32); If = sb("If", [128, 16, 16])
        cs2 = sb("cs2", [128, 8, 256]); eq = sb("eq", [128, 8, 256])
        BV = sb("BV", [128, 8, 16]); BP = sb("BP", [128, 8, 16], U32); BPa = sb("BPa", [128, 8, 16], U32); BPb = sb("BPb", [128, 8, 16], U32)
        apos = sb("apos", [128, 8, 16]); bpos = sb("bpos", [128, 8, 16]); gs = sb("gs", [128, 8, 16]); zz = sb("zz", [128, 8, 2])
        i0s = sb("i0s", [128, 8, 16]); i1s = sb("i1s", [128, 8, 16])
        selb = sb("selb", [128, 3, 128], BF16); selT = [sb(f"selT{i}", [128, 3, TT], BF16) for i in range(2)]
        ohA = [sb(f"ohA{i}", [128, 16, 64], BF16) for i in range(3)]
        ohB = [sb(f"ohB{i}", [128, 16, 128], BF16) for i in range(3)]
        Ug = [sb(f"Ug{i}", [128, G, 8, 128], BF16) for i in range(2)]
        Vg = [sb(f"Vg{i}", [128, G, D], BF16) for i in range(2)]
        ge = [sb(f"ge{i}", [128, TT], BF16) for i in range(2)]
        gW = [sb(f"gW{i}", [128, G, TT], BF16) for i in range(2)]
        r1 = sc[:, 0:8, :].rearrange("p a b -> p (a b)"); ho = sc[:, 8:16, :].rearrange("p a b -> p (a b)")
        wv = g["peer_wq"][l].rearrange("(kc p) n -> p kc n", p=128)
        for kc in range(8):
            i = 0
            op("sp", lambda e, kc=kc, i=i: e.dma_start(out=stg[i][:], in_=wv[:, kc, :]), writes=[f"stq{i}"], dma=f"stq{i}")
            op("act", lambda e, kc=kc, i=i: e.copy(out=Wq[:, kc, :], in_=stg[i][:]), reads=[f"stq{i}"], writes=[("Wq", kc)])
        WQK = [("Wq", kc) for kc in range(8)]
        op("dve", lambda e: e.memset(kst[:], 0.0), writes=["kst"])
        for i in range(2):
            op("sp", lambda e, i=i: e.dma_start(out=kst[i * 64:(i + 1) * 64, i * 128:(i + 1) * 128], in_=g["peer_keys"][l, i].rearrange("k d -> d k"), allow_slow_non_contiguous=True), reads=[], writes=["kst"], dma="c_kst")
        op("dve", lambda e: e.tensor_copy(out=KT[:], in_=kst[:]), reads=["kst"], writes=["KT"])
        op("sp", lambda e: e.dma_start(out=l2g[:], in_=g["ln2_g"][l].partition_broadcast(128)), writes=["l2g"], dma="c_l2g")
        op("sp", lambda e: e.dma_start(out=l2b[:], in_=g["ln2_b"][l].partition_broadcast(128)), writes=["l2b"], dma="c_l2b")
        V16v = V16[:].rearrange("p (h i) k -> p h i k", i=2)
        Ifv = If[:].rearrange("p (h i) k -> p h i k", i=2)
        cs4 = cs2[:].rearrange("p h (a b) -> p h a b", a=16)
        eq4 = eq[:].rearrange("p h (a b) -> p h a b", a=16)
        NTILE = NT // TT
        ng_ = NEB // G
        HK = lambda sl: [("h1T", sl, cc) for cc in range(NSUB)]
        WK = [("WT", cc) for cc in range(NSUB)]

        def sel_steps(tile):
            sl = tile % 2
            t0 = tile * TT
            steps = []
            for cc in range(NSUB):
                ci = (t0 + cc * 128) // 128
                rows = slice(ci * 128, (ci + 1) * 128)
                tcols = slice(cc * 128, (cc + 1) * 128)

                def s_load0(rows=rows, ci=ci, tcols=tcols, cc=cc):
                    op("sp", lambda e: e.dma_start(out=hq[:], in_=hs1[rows, :]), reads=[("hs1", ci)], writes=["hq"], dma="hq")
                    op("act", lambda e: e.copy(out=hqb[:], in_=hq[:]), reads=["hq"], writes=["hqb"])
                steps.append((False, False, s_load0))

                def s_load(rows=rows, ci=ci, tcols=tcols, cc=cc):
                    for kc in range(8):
                        op("pe", lambda e, kc=kc: e.transpose(out=PT[:, kc * 128:(kc + 1) * 128], in_=hqb[:, kc * 128:(kc + 1) * 128], identity=ident[:]), reads=["hqb", "ident"], writes=[("pt", kc)])
                    op("act", lambda e: e.copy(out=h1T[sl][:, :, tcols], in_=PTv), reads=[("pt", k) for k in range(8)], writes=[("h1T", sl, cc)])
                steps.append((True, True, s_load))

                def s_q(half, tcols=tcols, cc=cc):
                    def f():
                        for hd in range(4 * half, 4 * half + 4):
                            for kc in range(8):
                                op("pe", lambda e, hd=hd, kc=kc: e.matmul(PB[6][:, (hd % 4) * 128:(hd % 4 + 1) * 128], lhsT=Wq[:, kc, hd * 128:(hd + 1) * 128], rhs=h1T[sl][:, kc, tcols], start=(kc == 0), stop=(kc == 7)),
                                   reads=[("h1T", sl, cc)] + WQK, writes=[("pp", 6)])
                        op("act", lambda e: e.copy(out=qTs[:, 4 * half:4 * half + 4, :], in_=PB[6][:, :].rearrange("p (a b) -> p a b", a=4)), reads=[("pp", 6)], writes=[("qTs", half)])
                    return f
                steps.append((True, True, s_q(0))); steps.append((False, True, s_q(1)))

                def s_sc(j):
                    def f():
                        for hd in (2 * j, 2 * j + 1):
                            op("pe", lambda e, hd=hd: e.matmul(PB[6][:, (hd % 2) * 256:(hd % 2 + 1) * 256], lhsT=qTs[:, hd, :], rhs=KT[:], start=True, stop=True), reads=[("qTs", hd // 4), "KT"], writes=[("pp", 6)])
                        op("act", lambda e: e.copy(out=sc[:, 4 * j:4 * j + 4, :], in_=PB[6][:, :].rearrange("p (a b) -> p a b", a=4)), reads=[("pp", 6)], writes=[("sc", j), ("r1p", 0), ("r1p", 1), "hop"])
                    return f
                for j in range(4):
                    steps.append((j == 0, True, s_sc(j)))

                def s_top(hi):
                    def f():
                        sk = ("sc", hi // 4)
                        op("dve", lambda e: e.max(out=V16[:, hi, 0:8], in_=sc[:, hi, :]), reads=[sk], writes=["v8a"])
                        op("dve", lambda e: e.max_index(out=I16[:, hi, 0:8], in_max=V16[:, hi, 0:8], in_values=sc[:, hi, :]), reads=[sk, "v8a"], writes=["I16"])
                        op("dve", lambda e: e.match_replace(out=scr[:, 0:128], in_to_replace=V16[:, hi, 0:8], in_values=sc[:, hi, :], imm_value=-1e30), reads=[sk, "v8a"], writes=["scr"])
                        op("dve", lambda e: e.max(out=V16[:, hi, 8:16], in_=scr[:, 0:128]), reads=["scr"], writes=["v8b"])
                        op("dve", lambda e: e.max_index(out=I16[:, hi, 8:16], in_max=V16[:, hi, 8:16], in_values=scr[:, 0:128]), reads=["scr", "v8b"], writes=["I16"])
                    return f
                for hi in range(16):
                    steps.append((False, False, s_top(hi)))

                def s_cand():
                    op("dve", lambda e: e.tensor_copy(out=If[:], in_=I16[:]), reads=["I16"], writes=["If"])
                    op("dve", lambda e: e.tensor_tensor(out=cs4, in0=bc(V16v[:, :, 0, :].unsqueeze(3), [128, 8, 16, 16]), in1=bc(V16v[:, :, 1, :].unsqueeze(2), [128, 8, 16, 16]), op=ALU.add), reads=["v8a", "v8b"], writes=["cs2"])
                steps.append((False, False, s_cand))

                def s_top2(hd):
                    def f():
                        op("dve", lambda e: e.max(out=BV[:, hd, 0:8], in_=cs2[:, hd, :]), reads=["cs2"], writes=["b8a"])
                        op("dve", lambda e: e.max_index(out=BP[:, hd, 0:8], in_max=BV[:, hd, 0:8], in_values=cs2[:, hd, :]), reads=["cs2", "b8a"], writes=["BP"])
                        op("dve", lambda e: e.match_replace(out=scr[:, :], in_to_replace=BV[:, hd, 0:8], in_values=cs2[:, hd, :], imm_value=-1e30), reads=["cs2", "b8a"], writes=["scr"])
                        op("dve", lambda e: e.max(out=BV[:, hd, 8:16], in_=scr[:, :]), reads=["scr"], writes=["b8b"])
                        op("dve", lambda e: e.max_index(out=BP[:, hd, 8:16], in_max=BV[:, hd, 8:16], in_values=scr[:, :]), reads=["scr", "b8b"], writes=["BP"])
                    return f
                for hd in range(8):
                    steps.append((False, False, s_top2(hd)))

                def s_gate():
                    op("dve", lambda e: e.tensor_tensor(out=gs[:], in0=BV[:], in1=bc(BV[:, :, 0:1], [128, 8, 16]), op=ALU.subtract), reads=["b8a", "b8b"], writes=["gs"])
                    op("act", lambda e: e.activation(out=gs[:], in_=gs[:], func=AF.Exp), reads=["gs"], writes=["gs"])
                    op("dve", lambda e: e.reduce_sum(out=zz[:, :, 0], in_=gs[:], axis=AX.X), reads=["gs"], writes=["zz0"])
                    op("dve", lambda e: e.reciprocal(out=zz[:, :, 1], in_=zz[:, :, 0]), reads=["zz0"], writes=["zz1"])
                    op("dve", lambda e: e.tensor_tensor(out=gs[:], in0=gs[:], in1=bc(zz[:, :, 1:2], [128, 8, 16]), op=ALU.mult), reads=["gs", "zz1"], writes=["gs"])
                    op("dve", lambda e: e.tensor_single_scalar(out=BPa[:], in_=BP[:], scalar=4, op=ALU.logical_shift_right), reads=["BP"], writes=["BPa"])
                    op("dve", lambda e: e.tensor_single_scalar(out=BPb[:], in_=BP[:], scalar=15, op=ALU.bitwise_and), reads=["BP"], writes=["BPb"])
                    op("dve", lambda e: e.tensor_copy(out=apos[:], in_=BPa[:]), reads=["BPa"], writes=["apos"])
                    op("dve", lambda e: e.tensor_copy(out=bpos[:], in_=BPb[:]), reads=["BPb"], writes=["bpos"])
                steps.append((False, False, s_gate))

                def s_idx(which):
                    def f():
                        pos, pk, half, dst_, dk = ((apos, "apos", 0, i0s, "i0s"), (bpos, "bpos", 1, i1s, "i1s"))[which]
                        op("dve", lambda e: e.tensor_tensor(out=eq4, in0=bc(io16[:].unsqueeze(1).unsqueeze(1), [128, 8, 16, 16]), in1=bc(pos[:].unsqueeze(3), [128, 8, 16, 16]), op=ALU.is_equal), reads=["io16", pk], writes=["eq"])
                        op("dve", lambda e: e.tensor_tensor(out=eq4, in0=eq4, in1=bc(Ifv[:, :, half, :].unsqueeze(2), [128, 8, 16, 16]), op=ALU.mult), reads=["eq", "If"], writes=["eq"])
                        op("dve", lambda e: e.reduce_sum(out=dst_[:], in_=eq4, axis=AX.X), reads=["eq"], writes=[dk])
                    return f
                steps.append((False, False, s_idx(0))); steps.append((False, False, s_idx(1)))

                def s_selT0(tcols=tcols, cc=cc):
                    op("act", lambda e: e.copy(out=selb[:, 0, :], in_=i0s[:].rearrange("p h k -> p (h k)")), reads=["i0s"], writes=["selb0"])
                    op("act", lambda e: e.copy(out=selb[:, 1, :], in_=i1s[:].rearrange("p h k -> p (h k)")), reads=["i1s"], writes=["selb1"])
                    op("act", lambda e: e.copy(out=selb[:, 2, :], in_=gs[:].rearrange("p h k -> p (h k)")), reads=["gs"], writes=["selb2"])
                steps.append((False, False, s_selT0))

                def s_selT(tcols=tcols, cc=cc):
                    for k in range(3):
                        op("pe", lambda e, k=k: e.transpose(out=PT[:, k * 128:(k + 1) * 128], in_=selb[:, k, :], identity=ident[:]), reads=[f"selb{k}", "ident"], writes=[("pt", k)])
                    op("act", lambda e: e.copy(out=selT[sl][:, :, tcols], in_=PTv[:, 0:3, :]), reads=[("pt", k) for k in range(3)], writes=[("selT", sl, cc)])
                steps.append((True, True, s_selT))
            return steps

        nWc = [0]
        PTf = PT[:].bitcast(F32)

        def build_steps(tile, half):
            sl = tile % 2
            i0c = slice(half * 64, half * 64 + 64)
            pres, pes = [], []
            for cc in range(NSUB):
                for tb in range(8):
                    o_ = nWc[0] % 3
                    nWc[0] += 1
                    toks = slice(cc * 128 + tb * 16, cc * 128 + tb * 16 + 16)

                    def pre(cc=cc, o_=o_, toks=toks):
                        iob = bc(iocb[:].unsqueeze(1), [128, 16, 128])
                        ioh = bc(iocb[:, i0c].unsqueeze(1), [128, 16, 64])
                        op("dve", lambda e: e.tensor_tensor(out=ohA[o_][:], in0=ioh, in1=bc(selT[sl][:, 0, toks].unsqueeze(2), [128, 16, 64]), op=ALU.is_equal), reads=["iocb", ("selT", sl, cc)], writes=[f"ohA{o_}"])
                        op("dve", lambda e: e.tensor_tensor(out=ohA[o_][:], in0=ohA[o_][:], in1=bc(selT[sl][:, 2, toks].unsqueeze(2), [128, 16, 64]), op=ALU.mult), reads=[f"ohA{o_}", ("selT", sl, cc)], writes=[f"ohA{o_}"])
                        op("dve", lambda e: e.tensor_tensor(out=ohB[o_][:], in0=iob, in1=bc(selT[sl][:, 1, toks].unsqueeze(2), [128, 16, 128]), op=ALU.is_equal), reads=["iocb", ("selT", sl, cc)], writes=[f"ohB{o_}"])

                    def pe(cc=cc, tb=tb, o_=o_):
                        for q2 in range(2):
                            if q2 == 0:
                                dst_ps, pk = PTf, [("pt", k) for k in range(4)]
                            else:
                                dst_ps, pk = PB[6], [("pp", 6)]
                            for tt in range(8):
                                tl = q2 * 8 + tt
                                op("pe", lambda e, tl=tl, tt=tt, dst_ps=dst_ps: e.matmul(dst_ps[:, tt * 64:(tt + 1) * 64], lhsT=ohB[o_][:, tl, :], rhs=ohA[o_][:, tl, :], start=True, stop=True), reads=[f"ohA{o_}", f"ohB{o_}"], writes=pk)
                            tk0 = cc * 128 + tb * 16 + q2 * 8
                            op("act", lambda e, tk0=tk0, dst_ps=dst_ps: e.copy(out=WT[:, i0c, tk0:tk0 + 8].rearrange("p i t -> p t i"), in_=dst_ps[:, :].rearrange("p (t i) -> p t i", t=8)), reads=pk, writes=[("WT", half, cc)])
                    pres.append(pre); pes.append(pe)
            steps = []
            for k in range(0, len(pres), 2):
                steps.append((False, False, pres[k])); steps.append((False, False, pres[k + 1]))
                steps.append((True, True, pes[k])); steps.append((False, True, pes[k + 1]))
            return steps

        def group_steps(steps, max_nonpe):
            groups = [[]]
            n_nonpe = 0
            for newgrp, pe_first, fn in steps:
                if pe_first:
                    if newgrp or n_nonpe > 0:
                        if groups[-1]:
                            groups.append([])
                        n_nonpe = 0
                else:
                    if n_nonpe >= max_nonpe:
                        groups.append([])
                        n_nonpe = 0
                    n_nonpe += 1
                groups[-1].append((pe_first, fn))
            return groups

        nac = [0]

        def act_group(tile, gi):
            sl = tile % 2
            gs_ = gi % 2
            ebs = slice(gi * G, (gi + 1) * G)
            wk = [("WT", (gi * G) // 64, cc) for cc in range(NSUB)]
            op("sp", lambda e: e.dma_start(out=Ug[gs_][:], in_=UT[l, ebs].rearrange("g p k e -> p g k e")), reads=[("UT", l, eb) for eb in range(gi * G, (gi + 1) * G)], writes=[f"Ug{gs_}"], dma=f"Ug{gs_}")
            op("sp", lambda e: e.dma_start(out=Vg[gs_][:], in_=VB[l, ebs].rearrange("g p d -> p g d")), reads=[("VB", l, eb) for eb in range(gi * G, (gi + 1) * G)], writes=[f"Vg{gs_}"], dma=f"Vg{gs_}")
            for gl in range(G):
                eb = gi * G + gl
                ab = nac[0] % 2
                nac[0] += 1
                for kc in range(8):
                    op("pe", lambda e, gl=gl, kc=kc, ab=ab: e.matmul(PB[ab][:, 0:TT], lhsT=Ug[gs_][:, gl, kc, :], rhs=h1T[sl][:, kc, :], start=(kc == 0), stop=(kc == 7)), reads=[f"Ug{gs_}"] + HK(sl), writes=[("pp", ab)])
                op("act", lambda e, ab=ab: e.activation(out=ge[ab][:], in_=PB[ab][:, 0:TT], func=AF.Gelu), reads=[("pp", ab)], writes=[f"ge{ab}"])
                op("pool", lambda e, ab=ab, gl=gl, eb=eb: e.tensor_tensor(out=gW[gs_][:, gl, :], in0=ge[ab][:], in1=WT[:, eb, :], op=ALU.mult), reads=[f"ge{ab}"] + wk, writes=[(f"gW{gs_}", gl)])

        def out_group(tile, gi):
            gs_ = gi % 2
            for sub in range(NSUB):
                for hf in range(2):
                    bk = 2 + sub * 2 + hf
                    for gl in range(G):
                        op("pe", lambda e, gl=gl, sub=sub, hf=hf, bk=bk: e.matmul(PB[bk][:, :], lhsT=gW[gs_][:, gl, sub * 128:(sub + 1) * 128], rhs=Vg[gs_][:, gl, hf * 512:(hf + 1) * 512], start=(gi == 0 and gl == 0), stop=(gi == ng_ - 1 and gl == G - 1)),
                           reads=[(f"gW{gs_}", gl), f"Vg{gs_}"], writes=[("pp", bk)])

        def tail(tile):
            t0 = tile * TT
            for cc in range(NSUB):
                ci = (t0 + cc * 128) // 128
                rows = slice(ci * 128, (ci + 1) * 128)
                op("sp", lambda e, rows=rows: e.dma_start(out=hq[:], in_=hs1[rows, :]), reads=[("hs1", ci)], writes=["hq"], dma="hq")
                for hf in range(2):
                    op("dve", lambda e, cc=cc, hf=hf: e.scalar_tensor_tensor(out=r1[:, hf * 512:(hf + 1) * 512], in0=hq[:, hf * 512:(hf + 1) * 512], scalar=ALPHA, in1=PB[2 + cc * 2 + hf][:, :], op0=ALU.mult, op1=ALU.add),
                       reads=["hq", ("pp", 2 + cc * 2 + hf)], writes=[("r1p", hf)] + [("sc", j) for j in range(4)])
                ln_rows(lnst, r1, ho, l2g, l2b, "l2g", "l2b", "p", [("r1p", 0), ("r1p", 1)], ["hop"] + [("sc", j) for j in range(4)])
                op("sp", lambda e, rows=rows: e.dma_start(out=dst[rows, :], in_=ho[:]), reads=["hop"], writes=[("hs0", ci)], dma="hop")

        assert NSUB == 2 and ng_ % 2 == 0
        hg = ng_ // 2
        for _, _, f in sel_steps(0):
            f()
        for _, _, f in build_steps(0, 0):
            f()

        def fit(groups, n):
            while len(groups) > n:
                last = groups.pop()
                groups[-1].extend(last)
            return groups + [[] for _ in range(n - len(groups))]

        for tile in range(NTILE):
            has_next = tile + 1 < NTILE
            n1 = hg - 1
            chA = fit(group_steps(build_steps(tile, 1), 4), n1)
            chS = fit(group_steps(sel_steps(tile + 1), 10), n1) if has_next else [[] for _ in range(n1)]
            chB = fit(group_steps(build_steps(tile + 1, 0), 4), n1) if has_next else [[] for _ in range(n1)]
            act_group(tile, 0)
            for gi in range(ng_):
                if gi < n1:
                    todo = chA[gi] + chS[gi]
                elif hg <= gi < hg + n1:
                    todo = chB[gi - hg]
                else:
                    todo = []
                pes_ = [f for pe_first, f in todo if pe_first]
                for f in pes_[:1]:
                    f()
                if gi + 1 < ng_:
                    act_group(tile, gi + 1)
                for f in pes_[1:]:
                    f()
                out_group(tile, gi)
                for pe_first, f in todo:
                    if not pe_first:
                        f()
            tail(tile)


_NC_CACHE = {}


def kernel(**inputs):
    NCORES = 8
    x = np.asarray(inputs["x"], dtype=np.float32)
    B, S, _ = x.shape
    NSEQ = B // NCORES
    depth = inputs["w_in"].shape[0]
    key = (NSEQ, S, depth)
    if key not in _NC_CACHE:
        _NC_CACHE[key] = build_nc(NSEQ, S, depth, TT=256)
    nc = _NC_CACHE[key]
    in_maps = []
    for c in range(NCORES):
        m = {k: np.ascontiguousarray(np.asarray(v, dtype=np.float32)) for k, v in inputs.items() if k != "x"}
        m["x"] = np.ascontiguousarray(x[c * NSEQ:(c + 1) * NSEQ].reshape(NSEQ * S, D))
        in_maps.append(m)
    res = run_bass_kernel_spmd(nc, in_maps, core_ids=list(range(NCORES)))
    outs = [np.asarray(r["out"]).reshape(NSEQ, S, D) for r in res.results]
    return np.concatenate(outs, axis=0).astype(np.float32)
```
